# Optimizing a Trainium2 kernel written in Bass

```python
import math
import jax, jax.numpy as jnp
from jax import lax
import numpy as np


D_MODEL = 4096
BATCH = 4
SEQ = 2048
DEPTH = 1

ATTN_WIDTH = D_MODEL // 2
CONV_WIDTH = D_MODEL - ATTN_WIDTH
MIX_WIDTH = ATTN_WIDTH + CONV_WIDTH
HEAD_DIM = 128
V_HEAD_DIM = 2 * HEAD_DIM
N_DIFF_HEADS = ATTN_WIDTH // V_HEAD_DIM
CONV_GROUP = 128
N_CONV_GROUPS = CONV_WIDTH // CONV_GROUP
CONV_TAPS = 31
ROPE_THETA = 500000.0
ROPE_DIM = HEAD_DIM // 4
D_FF = -(-8 * D_MODEL // (3 * 256)) * 256
Q_BLOCK = 128
Q_COLS = N_DIFF_HEADS * 2 * HEAD_DIM
K_COLS = N_DIFF_HEADS * 2 * HEAD_DIM
V_COLS = N_DIFF_HEADS * V_HEAD_DIM
GLU_COLS = 2 * CONV_WIDTH
N_IN = Q_COLS + K_COLS + V_COLS + GLU_COLS
DEEPNORM_ALPHA = (2 * DEPTH) ** 0.25
DEEPNORM_BETA = (8 * DEPTH) ** -0.25
LN_EPS = 1e-5
MAX_POS_OFFSET = 1024

kernel_name = 'hybrid_diffattn_conformer_deepnorm'


def layer_norm(x, g, b):
    xf = x.astype(jnp.float32)
    mu = jnp.mean(xf, axis=-1, keepdims=True)
    xc = xf - mu
    var = jnp.mean(xc * xc, axis=-1, keepdims=True)
    y = xc * lax.rsqrt(var + LN_EPS)
    return (y * g.astype(jnp.float32) + b.astype(jnp.float32)).astype(x.dtype)


def rms_norm(x, g):
    xf = x.astype(jnp.float32)
    y = xf * lax.rsqrt(jnp.mean(xf * xf, axis=-1, keepdims=True) + LN_EPS)
    return (y * g.astype(jnp.float32)).astype(x.dtype)


def partial_rotary(t, cos, sin):
    half = ROPE_DIM // 2
    t1 = t[..., :half]
    t2 = t[..., half:ROPE_DIM]
    rot = jnp.concatenate([t1 * cos - t2 * sin, t2 * cos + t1 * sin], axis=-1)
    return jnp.concatenate([rot, t[..., ROPE_DIM:]], axis=-1)


def diff_attention(q1, q2, k1, k2, v, lam):
    B, H, S, _ = q1.shape
    nb = S // Q_BLOCK
    scale = HEAD_DIM ** -0.5
    qb1 = jnp.moveaxis(q1.reshape(B, H, nb, Q_BLOCK, HEAD_DIM), 2, 0)
    qb2 = jnp.moveaxis(q2.reshape(B, H, nb, Q_BLOCK, HEAD_DIM), 2, 0)
    kpos = jnp.arange(S)
    vf = v.astype(jnp.float32)

    def block(args):
        i, qa, qb = args
        qpos = i * Q_BLOCK + jnp.arange(Q_BLOCK)
        mask = kpos[None, :] <= qpos[:, None]
        s1 = jnp.einsum('bhqd,bhkd->bhqk', qa, k1).astype(jnp.float32) * scale
        s2 = jnp.einsum('bhqd,bhkd->bhqk', qb, k2).astype(jnp.float32) * scale
        p1 = jax.nn.softmax(jnp.where(mask, s1, -jnp.inf), axis=-1)
        p2 = jax.nn.softmax(jnp.where(mask, s2, -jnp.inf), axis=-1)
        return jnp.einsum('bhqk,bhkd->bhqd', p1 - lam * p2, vf)

    out = lax.map(block, (jnp.arange(nb), qb1, qb2))
    out = jnp.moveaxis(out, 0, 2).reshape(B, H, S, V_HEAD_DIM)
    return jnp.transpose(out, (0, 2, 1, 3)).astype(v.dtype)


def hybrid_mixer(x, cos, sin, w_in, b_glu, conv_w, conv_b, conv_ln_g, conv_ln_b,
                 lam_q1, lam_k1, lam_q2, lam_k2, subln_g, w_o, lambda_init):
    B, S, _ = x.shape
    proj = jnp.einsum('bsd,dn->bsn', x, w_in)
    q = proj[..., :Q_COLS].reshape(B, S, N_DIFF_HEADS, 2, HEAD_DIM)
    k = proj[..., Q_COLS:Q_COLS + K_COLS].reshape(B, S, N_DIFF_HEADS, 2, HEAD_DIM)
    v = proj[..., Q_COLS + K_COLS:Q_COLS + K_COLS + V_COLS].reshape(B, S, N_DIFF_HEADS, V_HEAD_DIM)
    u = proj[..., Q_COLS + K_COLS + V_COLS:]

    c5 = cos[:, :, None, None, :]
    s5 = sin[:, :, None, None, :]
    q = partial_rotary(q, c5, s5)
    k = partial_rotary(k, c5, s5)
    q1 = jnp.transpose(q[..., 0, :], (0, 2, 1, 3))
    q2 = jnp.transpose(q[..., 1, :], (0, 2, 1, 3))
    k1 = jnp.transpose(k[..., 0, :], (0, 2, 1, 3))
    k2 = jnp.transpose(k[..., 1, :], (0, 2, 1, 3))
    vh = jnp.transpose(v, (0, 2, 1, 3))
    lam = (jnp.exp(jnp.sum(lam_q1.astype(jnp.float32) * lam_k1.astype(jnp.float32)))
           - jnp.exp(jnp.sum(lam_q2.astype(jnp.float32) * lam_k2.astype(jnp.float32)))
           + lambda_init)
    o = diff_attention(q1, q2, k1, k2, vh, lam)
    o = rms_norm(o, subln_g) * (1.0 - lambda_init)
    attn_out = o.reshape(B, S, ATTN_WIDTH)

    u = u + b_glu
    a, gate = u[..., :CONV_WIDTH], u[..., CONV_WIDTH:]
    c = a * jax.nn.sigmoid(gate)
    cp = jnp.pad(c, ((0, 0), (CONV_TAPS - 1, 0), (0, 0)))
    c = lax.conv_general_dilated(cp, conv_w[:, None, :], window_strides=(1,), padding='VALID',
                                 dimension_numbers=('NWC', 'WIO', 'NWC'),
                                 feature_group_count=CONV_WIDTH) + conv_b
    c = jax.nn.silu(layer_norm(c, conv_ln_g, conv_ln_b))

    mix = jnp.concatenate([attn_out, c], axis=-1)
    return jnp.einsum('bsm,md->bsd', mix, w_o)


def swiglu_ffn(h, w_gate, w_up, w_down):
    hid = jax.nn.silu(jnp.einsum('bsd,df->bsf', h, w_gate)) * jnp.einsum('bsd,df->bsf', h, w_up)
    return jnp.einsum('bsf,fd->bsd', hid, w_down)


def setup_inputs(seed: int = 0) -> dict:
    key = jax.random.key(seed)
    ks = jax.random.split(key, 24)
    f32 = jnp.float32
    L, D, F = DEPTH, D_MODEL, D_FF

    def nrm(k, shape, scale):
        return jax.random.normal(k, shape, f32) * scale

    x = nrm(ks[0], (BATCH, SEQ, D), 1.0)
    offset = jax.random.randint(ks[1], (BATCH, 1), 0, MAX_POS_OFFSET, dtype=jnp.int32)
    positions = offset + jnp.arange(SEQ, dtype=jnp.int32)[None, :]
    w_in = nrm(ks[2], (L, D, N_IN), D ** -0.5)
    b_glu = nrm(ks[3], (L, GLU_COLS), 0.02)
    conv_w = nrm(ks[4], (L, CONV_TAPS, CONV_WIDTH), CONV_TAPS ** -0.5)
    conv_b = nrm(ks[5], (L, CONV_WIDTH), 0.02)
    conv_ln_g = 1.0 + nrm(ks[6], (L, CONV_WIDTH), 0.02)
    conv_ln_b = nrm(ks[7], (L, CONV_WIDTH), 0.02)
    lam_q1 = nrm(ks[8], (L, HEAD_DIM), 0.1)
    lam_k1 = nrm(ks[9], (L, HEAD_DIM), 0.1)
    lam_q2 = nrm(ks[10], (L, HEAD_DIM), 0.1)
    lam_k2 = nrm(ks[11], (L, HEAD_DIM), 0.1)
    subln_g = 1.0 + nrm(ks[12], (L, V_HEAD_DIM), 0.02)
    w_o = nrm(ks[13], (L, MIX_WIDTH, D), MIX_WIDTH ** -0.5 * DEEPNORM_BETA)
    ln1_g = 1.0 + nrm(ks[14], (L, D), 0.02)
    ln1_b = nrm(ks[15], (L, D), 0.02)
    w_gate = nrm(ks[16], (L, D, F), D ** -0.5)
    w_up = nrm(ks[17], (L, D, F), D ** -0.5)
    w_down = nrm(ks[18], (L, F, D), F ** -0.5 * DEEPNORM_BETA)
    ln2_g = 1.0 + nrm(ks[19], (L, D), 0.02)
    ln2_b = nrm(ks[20], (L, D), 0.02)
    return {'x': x, 'positions': positions, 'w_in': w_in, 'b_glu': b_glu,
            'conv_w': conv_w, 'conv_b': conv_b, 'conv_ln_g': conv_ln_g, 'conv_ln_b': conv_ln_b,
            'lam_q1': lam_q1, 'lam_k1': lam_k1, 'lam_q2': lam_q2, 'lam_k2': lam_k2,
            'subln_g': subln_g, 'w_o': w_o, 'ln1_g': ln1_g, 'ln1_b': ln1_b,
            'w_gate': w_gate, 'w_up': w_up, 'w_down': w_down, 'ln2_g': ln2_g, 'ln2_b': ln2_b}


def reference(x, positions, w_in, b_glu, conv_w, conv_b, conv_ln_g, conv_ln_b,
              lam_q1, lam_k1, lam_q2, lam_k2, subln_g, w_o, ln1_g, ln1_b,
              w_gate, w_up, w_down, ln2_g, ln2_b):
    inv_freq = ROPE_THETA ** (-jnp.arange(0, ROPE_DIM, 2, dtype=jnp.float32) / ROPE_DIM)
    ang = positions.astype(jnp.float32)[..., None] * inv_freq
    cos = jnp.cos(ang).astype(x.dtype)
    sin = jnp.sin(ang).astype(x.dtype)
    for l in range(DEPTH):
        lambda_init = 0.8 - 0.6 * math.exp(-0.3 * l)
        mix = hybrid_mixer(x, cos, sin, w_in[l], b_glu[l], conv_w[l], conv_b[l],
                           conv_ln_g[l], conv_ln_b[l], lam_q1[l], lam_k1[l], lam_q2[l], lam_k2[l],
                           subln_g[l], w_o[l], lambda_init)
        h = layer_norm(DEEPNORM_ALPHA * x + mix, ln1_g[l], ln1_b[l])
        x = layer_norm(DEEPNORM_ALPHA * h + swiglu_ffn(h, w_gate[l], w_up[l], w_down[l]), ln2_g[l], ln2_b[l])
    return x
```

```python
import math
import numpy as np
import concourse.bass as bass
import concourse.mybir as mybir
from concourse.bass_utils import run_bass_kernel_spmd

F32 = mybir.dt.float32
BF16 = mybir.dt.bfloat16
I32 = mybir.dt.int32
AF = mybir.ActivationFunctionType
ALU = mybir.AluOpType
AX = mybir.AxisListType

D = 4096
S = 2048
B = 4
NT = 1024
DFF = 11008
NFB = DFF // 128
NS = 8
FBG = 4
ALPHA = 2.0 ** 0.25
LAMBDA_INIT = 0.8 - 0.6 * math.exp(0.0)
EPS = 1e-5
SM_SCALE = 128.0 ** -0.5

W_Q = 0
W_K = 16
W_V = 32
W_A = 48
W_G = 64
W_O = 80
W_GT = 112
W_UP = 112 + NFB
W_DW = 112 + 2 * NFB
NSLOT = 112 + 3 * NFB

C_IDENT = 0
C_TRI = 128
C_BGLU = 256
C_CONVW = C_BGLU + 32
C_CONVB = C_CONVW + 16 * 31
C_CLNG = C_CONVB + 16
C_CLNB = C_CLNG + 16
C_LN1G = C_CLNB + 16
C_LN1B = C_LN1G + 32
C_LN2G = C_LN1B + 32
C_LN2B = C_LN2G + 32
C_FLAGS = C_LN2B + 32
C_LAMV = C_FLAGS + 2
C_SUBG = C_LAMV + 512
C_INVF = C_SUBG + 256
NCST = C_INVF + 16


class Buf:
    __slots__ = ("name", "writer", "readers", "sem", "cnt")

    def __init__(self, name):
        self.name = name
        self.writer = None
        self.readers = []
        self.sem = None
        self.cnt = 0


class Prog:
    def __init__(self, nc):
        self.nc = nc
        self.eng = ["pe", "act", "dve", "pool", "sp"]
        self.q = {e: [] for e in self.eng}
        self.esem = {e: nc.alloc_semaphore(name="s_" + e) for e in ("pe", "act", "dve", "pool")}
        self.ecnt = {e: 0 for e in ("pe", "act", "dve", "pool")}
        self.waited = {e: {} for e in self.eng}
        self.pending = {e: [] for e in self.eng}
        self.dmabufs = []
        self.nsem = 3
        self.pool_bar = []

    def buf(self, name, dma=False):
        b = Buf(name)
        if dma:
            b.sem = self.nc.alloc_semaphore(name="d_" + name)
            self.nsem += 1
            self.dmabufs.append(b)
        return b

    def _waits(self, eng, toks):
        out = []
        toks = list(toks) + self.pending[eng]
        self.pending[eng] = []
        w = self.waited[eng]
        best = {}
        for t in toks:
            if t is None:
                continue
            key, sem, val, src = t
            if src == eng and eng == "pe":
                continue
            if w.get(key, 0) >= val:
                continue
            if key not in best or best[key][1] < val:
                best[key] = (sem, val)
        for key, (sem, val) in best.items():
            w[key] = val
            out.append((sem, val))
        return out

    def _deps(self, reads, writes):
        toks = []
        for b in reads:
            toks.append(b.writer)
        for b in writes:
            toks.append(b.writer)
            toks.extend(b.readers)
        return toks

    def _update(self, tok, reads, writes):
        for b in reads:
            b.readers.append(tok)
        for b in writes:
            b.writer = tok
            b.readers = []

    def op(self, eng, fn, reads=(), writes=()):
        toks = self._deps(reads, writes)
        waits = self._waits(eng, toks)
        self.ecnt[eng] += 1
        tok = ("e_" + eng, self.esem[eng], self.ecnt[eng], eng)
        self.q[eng].append((waits, [fn], (self.esem[eng], 1)))
        self._update(tok, reads, writes)

    def snapshot(self):
        toks = []
        for e in ("pe", "act", "dve", "pool"):
            if self.ecnt[e]:
                toks.append(("e_" + e, self.esem[e], self.ecnt[e], "bar"))
        for b in self.dmabufs:
            if b.cnt:
                toks.append(("d_" + b.name, b.sem, b.cnt, "bar"))
        return toks

    def dma(self, queue, fns, reads, writes, owner, phase_mem=False, extra=()):
        toks = self._deps(reads, writes) + list(extra)
        if phase_mem:
            toks = toks + self.pool_bar
        waits = self._waits(queue, toks)
        owner.cnt += 16 * len(fns)
        tok = ("d_" + owner.name, owner.sem, owner.cnt, "dma")
        self.q[queue].append((waits, list(fns), (owner.sem, 16)))
        self._update(tok, reads, writes)

    def barrier(self, pool=False):
        toks = []
        for e in ("pe", "act", "dve", "pool"):
            if self.ecnt[e]:
                toks.append(("e_" + e, self.esem[e], self.ecnt[e], "bar"))
        for b in self.dmabufs:
            if b.cnt:
                toks.append(("d_" + b.name, b.sem, b.cnt, "bar"))
        for e in self.eng:
            if e == "pool" and not pool:
                continue
            self.pending[e] = list(toks)
        self.pool_bar = list(toks)

    def barrier_partial(self, engines, skip_bufs):
        toks = []
        for e in engines:
            if self.ecnt[e]:
                toks.append(("e_" + e, self.esem[e], self.ecnt[e], "bar"))
        for b in self.dmabufs:
            if b.cnt and b not in skip_bufs:
                toks.append(("d_" + b.name, b.sem, b.cnt, "bar"))
        for e in self.eng:
            if e == "pool":
                continue
            self.pending[e] = self.pending[e] + list(toks)
        self.pool_bar = self.pool_bar + list(toks)

    def barrier_targets(self, toks, targets):
        for e in targets:
            self.pending[e] = self.pending[e] + list(toks)

    def replay(self, name, e):
        for waits, fns, inc in self.q[name]:
            for sem, val in waits:
                e.wait_ge(sem, val)
            for fn in fns:
                ins = fn(e)
                ins.then_inc(inc[0], inc[1])
        for sem, val in self._waits(name, []):
            e.wait_ge(sem, val)


def build_program(debug=False, stop_after=None):
    nc = bass.Bass("TRN2", target_bir_lowering=False)
    P = Prog(nc)
    skind = "ExternalOutput" if debug else "Internal"

    wst = nc.dram_tensor("wst", [NSLOT * 128, 4096], F32, kind="ExternalInput").ap()
    xto = nc.dram_tensor("xto", [D, NT], F32, kind="ExternalInput").ap()
    xtp = nc.dram_tensor("xtp", [D, NT], F32, kind="ExternalInput").ap()
    xth = nc.dram_tensor("xth", [D, 32], F32, kind="ExternalInput").ap()
    posi = nc.dram_tensor("posi", [128, 16], I32, kind="ExternalInput").ap()
    cst = nc.dram_tensor("cst", [128, NCST], F32, kind="ExternalInput").ap()
    yt = nc.dram_tensor("yt", [D, NT], F32, kind="ExternalOutput").ap()
    qts = nc.dram_tensor("qts", [16, 128, NT], BF16, kind=skind).ap()
    kts = nc.dram_tensor("kts", [16, 128, 2 * NT], BF16, kind=skind).ap()
    vsd = nc.dram_tensor("vsd", [8, 16, 128, 258], BF16, kind=skind).ap()
    cvd = nc.dram_tensor("cvd", [16, 128, NT], BF16, kind=skind).ap()
    mts = nc.dram_tensor("mts", [32, 128, NT], BF16, kind=skind).ap()

    base = (nc.sbuf_base + 63) // 64 * 64
    top = nc.sbuf_top
    cur = [base]

    def alloc(name, shape, dt, at=None):
        nbytes = int(np.prod(shape[1:])) * (4 if dt in (F32, I32) else 2)
        nbytes = (nbytes + 63) // 64 * 64
        if at is None:
            off = cur[0]
            cur[0] += nbytes
        else:
            off = at[0]
            at[0] += nbytes
        assert off + nbytes <= top, (name, off, nbytes, top)
        return nc.alloc_sbuf_tensor_at(name, list(shape), dt, offset=off)

    SL = [alloc(f"sl{i}", [128, 4096], BF16) for i in range(NS)]
    SLb = [P.buf(f"sl{i}", dma=True) for i in range(NS)]
    CST = alloc("cst_sb", [128, NCST], F32)
    CSTb = P.buf("cst", dma=True)
    POSI = alloc("posi_sb", [128, 16], I32)
    IDENT = alloc("ident", [128, 128], BF16)
    TRI = alloc("tri", [128, 128], BF16)
    ONESB = alloc("onesb", [128, 128], BF16)
    ONESF = alloc("onesf", [128, 128], F32)
    ONESC = alloc("onesc", [128, 128], BF16)
    COS4 = alloc("cos4", [128, 16, 4, 16], F32)
    SIN4 = alloc("sin4", [128, 16, 4, 16], F32)
    NLAM = alloc("nlam", [128, 1], F32)
    G8 = alloc("g8", [128, 256], F32)
    AG1 = alloc("ag1", [128, 32], F32)
    AB1 = alloc("ab1", [128, 32], F32)
    EPSC = alloc("epsc", [128, 1], F32)
    SETb = P.buf("setup")
    phase_base = cur[0]

    PBt = [nc.alloc_psum_tensor(f"pb{i}", [128, 512], F32) for i in range(6)]
    PBb = [P.buf(f"pb{i}") for i in range(6)]
    TBt = [nc.alloc_psum_tensor(f"tb{i}", [128, 1024], BF16) for i in range(2)]
    TBb = [P.buf(f"tb{i}") for i in range(2)]
    pbi = [0]
    tbi = [0]

    pinned = set()

    def next_pb():
        while True:
            i = pbi[0] % 6
            pbi[0] += 1
            if i not in pinned:
                return i

    def next_tb():
        i = tbi[0] % 2
        tbi[0] += 1
        return i

    nload = [0]

    def load_slot(widx):
        s = nload[0] % NS
        nload[0] += 1
        fns = []
        for hh in range(2):
            def fn(e, hh=hh, s=s, widx=widx):
                return e.dma_start(out=SL[s][:, hh * 2048:(hh + 1) * 2048],
                                   in_=wst[widx * 128:(widx + 1) * 128, hh * 2048:(hh + 1) * 2048])
            fns.append(fn)
        P.dma("pool", fns, reads=[], writes=[SLb[s]], owner=SLb[s])
        return s

    pa = [phase_base]
    XT = alloc("xt", [128, 32, NT], BF16, pa)
    XTq = [P.buf(f"xtq{k}", dma=True) for k in range(4)]
    pa1 = [pa[0]]
    STG = [alloc(f"stg{i}", [128, 4128], BF16, pa1) for i in range(2)]
    STGb = [P.buf(f"stg{i}", dma=True) for i in range(2)]
    KB = [alloc(f"kb{i}", [128, 512], BF16, pa1) for i in range(2)]
    KBb = [P.buf(f"kb{i}") for i in range(2)]
    RT = [[alloc(f"rt{i}_{k}", [128, 4, 16], F32, pa1) for k in range(4)] for i in range(2)]
    RTb = [[P.buf(f"rt{i}_{k}") for k in range(4)] for i in range(2)]

    P.dma("sp", [lambda e: e.dma_start(out=CST[:, :], in_=cst),
                 lambda e: e.dma_start(out=POSI[:, :], in_=posi)], reads=[], writes=[CSTb], owner=CSTb)
    sat = [pa1[0]]
    POSF = alloc("posf", [128, 16], F32, sat)
    ANG = alloc("ang", [128, 16, 16], F32, sat)
    V1 = alloc("v1", [128, 256], F32, sat)
    VI = alloc("vi", [128, 256], I32, sat)
    VF = alloc("vf", [128, 256], F32, sat)
    RR_ = alloc("rr", [128, 256], F32, sat)
    GT_ = alloc("gt", [128, 256], F32, sat)
    CS = alloc("cs", [128, 16, 16], F32, sat)
    PR = alloc("pr", [128, 128], F32, sat)
    DD = alloc("dd", [128, 4], F32, sat)
    Sb = P.buf("setup_tmp")
    R = [CSTb, Sb, SETb]
    Wr = [Sb, SETb]

    def sop(eng, fn):
        P.op(eng, fn, reads=R, writes=Wr)

    sop("dve", lambda e: e.tensor_copy(out=IDENT[:, :], in_=CST[:, C_IDENT:C_IDENT + 128]))
    sop("dve", lambda e: e.tensor_copy(out=TRI[:, :], in_=CST[:, C_TRI:C_TRI + 128]))
    sop("dve", lambda e: e.memset(ONESB[:, :], 1.0))
    sop("dve", lambda e: e.memset(ONESC[:, :], 1.0 / 2048.0))
    sop("dve", lambda e: e.memset(ONESF[:, :], 1.0 / D))
    sop("dve", lambda e: e.memset(EPSC[:, :], EPS))
    sop("dve", lambda e: e.tensor_copy(out=POSF[:, :], in_=POSI[:, :]))
    for j in range(16):
        sop("dve", lambda e, j=j: e.tensor_scalar(ANG[:, j, :], CST[:, C_INVF:C_INVF + 16], POSF[:, j:j + 1], None, ALU.mult))
    angf = ANG[:, :, :].rearrange("p a b -> p (a b)")
    for which, shift, DST in ((0, 0.0, SIN4), (1, 0.25, COS4)):
        sop("dve", lambda e, shift=shift: e.tensor_scalar(V1[:, :], angf, 1.0 / (2 * math.pi), shift, ALU.mult, ALU.add))
        sop("dve", lambda e: e.tensor_copy(out=VI[:, :], in_=V1[:, :]))
        sop("dve", lambda e: e.tensor_copy(out=VF[:, :], in_=VI[:, :]))
        sop("dve", lambda e: e.tensor_tensor(out=RR_[:, :], in0=V1[:, :], in1=VF[:, :], op=ALU.subtract))
        sop("dve", lambda e: e.tensor_scalar(GT_[:, :], RR_[:, :], 0.5, None, ALU.is_gt))
        sop("dve", lambda e: e.tensor_tensor(out=RR_[:, :], in0=RR_[:, :], in1=GT_[:, :], op=ALU.subtract))
        sop("dve", lambda e: e.tensor_scalar(GT_[:, :], RR_[:, :], -0.5, None, ALU.is_lt))
        sop("dve", lambda e: e.tensor_tensor(out=RR_[:, :], in0=RR_[:, :], in1=GT_[:, :], op=ALU.add))
        sop("act", lambda e: e.activation(out=CS[:, :, :].rearrange("p a b -> p (a b)"), in_=RR_[:, :], func=AF.Sin,
                                          scale=2 * math.pi * (1 - 2e-6)))
        for c in range(4):
            sop("dve", lambda e, c=c, DST=DST: e.tensor_copy(out=DST[:, :, c, :], in_=CS[:, :, :]))
    for k in range(2):
        sop("dve", lambda e, k=k: e.tensor_tensor(out=PR[:, :], in0=CST[:, C_LAMV + 256 * k:C_LAMV + 256 * k + 128],
                                                  in1=CST[:, C_LAMV + 256 * k + 128:C_LAMV + 256 * k + 256], op=ALU.mult))
        sop("dve", lambda e, k=k: e.reduce_sum(out=DD[:, k:k + 1], in_=PR[:, :], axis=AX.X))
    sop("act", lambda e: e.activation(out=DD[:, 2:4], in_=DD[:, 0:2], func=AF.Exp))
    sop("dve", lambda e: e.tensor_tensor(out=NLAM[:, :], in0=DD[:, 3:4], in1=DD[:, 2:3], op=ALU.subtract))
    sop("dve", lambda e: e.tensor_scalar(NLAM[:, :], NLAM[:, :], -LAMBDA_INIT, None, ALU.add))
    sop("dve", lambda e: e.tensor_scalar(G8[:, :], CST[:, C_SUBG:C_SUBG + 256], 1.0 - LAMBDA_INIT, None, ALU.mult))
    sop("dve", lambda e: e.tensor_scalar(AG1[:, :], CST[:, C_LN1G:C_LN1G + 32], ALPHA, None, ALU.mult))
    sop("dve", lambda e: e.tensor_scalar(AB1[:, :], CST[:, C_LN1B:C_LN1B + 32], ALPHA, None, ALU.mult))
    KR = [CSTb, SETb]

    def load_xt(src, widx0=None):
        v = src.rearrange("(c p) t -> p c t", p=128)
        slots = []
        for k in range(4):
            if widx0 is not None:
                slots.append(load_slot(widx0 + k))
            fns = []
            for c2 in range(4 * k, 4 * k + 4):
                fns.append(lambda e, c2=c2: e.dma_start(out=XT[:, 2 * c2:2 * c2 + 2, :], in_=v[:, 2 * c2:2 * c2 + 2, :]))
            P.dma("pool", fns, reads=[], writes=[XTq[k]], owner=XTq[k], phase_mem=True)
        return slots

    stgi = [0]
    kbi = [0]

    def proj_x(slots, tt):
        bk = next_pb()
        for k in range(4):
            def fn(e, k=k):
                for kc in range(8 * k, 8 * k + 8):
                    ins = e.matmul(PBt[bk][:, :], lhsT=XT[:, kc, tt * 128:(tt + 1) * 128],
                                   rhs=SL[slots[k]][:, (kc % 8) * 512:(kc % 8 + 1) * 512],
                                   start=(kc == 0), stop=(kc == 31))
                return ins
            P.op("pe", fn, reads=[XTq[k], SLb[slots[k]]], writes=[PBb[bk]])
        return bk

    def proj_x_qmajor(slots, tts):
        bks = [next_pb() for _ in tts]
        for k in range(4):
            for tt, bk in zip(tts, bks):
                def fn(e, k=k, tt=tt, bk=bk):
                    for kc in range(8 * k, 8 * k + 8):
                        ins = e.matmul(PBt[bk][:, :], lhsT=XT[:, kc, tt * 128:(tt + 1) * 128],
                                       rhs=SL[slots[k]][:, (kc % 8) * 512:(kc % 8 + 1) * 512],
                                       start=(kc == 0), stop=(kc == 31))
                    return ins
                P.op("pe", fn, reads=[XTq[k], SLb[slots[k]]], writes=[PBb[bk]])
        return bks

    def rope_evac(bk, j):
        x = kbi[0] % 2
        kbi[0] += 1
        v = PBt[bk][:, :].rearrange("p (c d) -> p c d", c=4)
        kb = KB[x][:, :].rearrange("p (c d) -> p c d", c=4)
        t1, t2 = v[:, :, 0:16], v[:, :, 16:32]
        cs, sn = COS4[:, j, :, :], SIN4[:, j, :, :]
        rt, rtb = RT[x], RTb[x]
        P.op("dve", lambda e: e.tensor_tensor(out=rt[0][:, :, :], in0=t1, in1=cs, op=ALU.mult), reads=[PBb[bk]] + KR, writes=[rtb[0]])
        P.op("dve", lambda e: e.tensor_tensor(out=rt[1][:, :, :], in0=t2, in1=sn, op=ALU.mult), reads=[PBb[bk]] + KR, writes=[rtb[1]])
        P.op("dve", lambda e: e.tensor_tensor(out=rt[2][:, :, :], in0=t2, in1=cs, op=ALU.mult), reads=[PBb[bk]] + KR, writes=[rtb[2]])
        P.op("dve", lambda e: e.tensor_tensor(out=rt[3][:, :, :], in0=t1, in1=sn, op=ALU.mult), reads=[PBb[bk]] + KR, writes=[rtb[3]])
        P.op("dve", lambda e: e.tensor_tensor(out=kb[:, :, 0:16], in0=rt[0][:, :, :], in1=rt[1][:, :, :], op=ALU.subtract),
             reads=[rtb[0], rtb[1]], writes=[KBb[x]])
        P.op("dve", lambda e: e.tensor_tensor(out=kb[:, :, 16:32], in0=rt[2][:, :, :], in1=rt[3][:, :, :], op=ALU.add),
             reads=[rtb[2], rtb[3]], writes=[KBb[x]])
        P.op("act", lambda e: e.activation(out=kb[:, :, 32:128], in_=v[:, :, 32:128], func=AF.Copy),
             reads=[PBb[bk]], writes=[KBb[x]])
        return x

    def transposes_to_stage(x, y, tt):
        tb = next_tb()

        def fn(e):
            for c in range(4):
                ins = e.transpose(out=TBt[tb][:, c * 128:(c + 1) * 128], in_=KB[x][:, c * 128:(c + 1) * 128], identity=IDENT[:, :])
            return ins
        P.op("pe", fn, reads=[KBb[x]] + KR, writes=[TBb[tb]])
        dst = STG[y][:, 0:4096].rearrange("p (c t) -> p c t", c=4)[:, :, tt * 128:(tt + 1) * 128]
        src = TBt[tb][:, 0:512].rearrange("p (c t) -> p c t", c=4)
        P.op("act", lambda e: e.activation(out=dst, in_=src, func=AF.Copy), reads=[TBb[tb]], writes=[STGb[y]])

    def qk_panel(widx0, dst, comp0, sidx, tcol0, slots=None, qmajor=False):
        if slots is None:
            slots = [load_slot(widx0 + kq) for kq in range(4)]
        y = stgi[0] % 2
        stgi[0] += 1
        prev = None
        pre_bks = proj_x_qmajor(slots, list(range(6))) if qmajor else []
        for tt in range(8):
            bk = pre_bks[tt] if tt < len(pre_bks) else proj_x(slots, tt)
            if prev is not None:
                transposes_to_stage(prev[0], y, prev[1])
            x = rope_evac(bk, sidx * 8 + tt)
            prev = (x, tt)
            drain(max(0, -(-(len(conv_q) - 24) // (8 - tt))))
        transposes_to_stage(prev[0], y, prev[1])
        dv = dst[comp0:comp0 + 4, :, tcol0:tcol0 + NT].rearrange("c d t -> d c t")
        P.dma("sp", [lambda e: e.dma_start(out=dv, in_=STG[y][:, 0:4096].rearrange("p (c t) -> p c t", c=4))],
              reads=[STGb[y]], writes=[], owner=STGb[y])

    def v_panel(widx0, h0, sidx):
        slots = [load_slot(widx0 + kq) for kq in range(4)]
        y = stgi[0] % 2
        stgi[0] += 1
        sv = STG[y][:, :].rearrange("p (t h c) -> p t h c", t=8, h=2)
        for tt in range(8):
            bk = proj_x(slots, tt)
            P.op("act", lambda e, bk=bk, tt=tt: e.activation(out=sv[:, tt, :, 0:256],
                                                             in_=PBt[bk][:, :].rearrange("p (h c) -> p h c", h=2), func=AF.Copy),
                 reads=[PBb[bk]], writes=[STGb[y]])
            for hh in range(2):
                P.op("dve", lambda e, tt=tt, hh=hh: e.tensor_copy(out=sv[:, tt, hh, 256:257],
                                                                  in_=CST[:, C_FLAGS + sidx:C_FLAGS + sidx + 1]),
                     reads=KR, writes=[STGb[y]])
            drain(max(0, -(-(len(conv_q) - 24) // (8 - tt))))
        P.dma("sp", [lambda e, hh=hh: e.dma_start(out=vsd[h0 + hh, sidx * 8:(sidx + 1) * 8, :, 0:257].rearrange("t p c -> p t c"),
                                                   in_=sv[:, :, hh, 0:257]) for hh in range(2)],
              reads=[STGb[y]], writes=[], owner=STGb[y])

    pa2 = [sat[0]]
    pa2_start = sat[0]
    XTH = alloc("xth_sb", [128, 32, 32], BF16, pa2)
    XTHb = P.buf("xth", dma=True)
    CC = [alloc(f"cc{i}", [128, 1056], F32, pa2) for i in range(2)]
    CCb = [P.buf(f"cc{i}") for i in range(2)]
    SG = [alloc(f"sg{i}", [128, 1056], F32, pa2) for i in range(2)]
    SGb = [P.buf(f"sg{i}") for i in range(2)]
    ACC2 = [alloc(f"acc2_{i}", [128, 1024], F32, pa2) for i in range(2)]
    ACC2b = [P.buf(f"acc2_{i}") for i in range(2)]
    CVS = [alloc(f"cvs{i}", [128, 1024], BF16, pa2) for i in range(2)]
    CVSb = [P.buf(f"cvs{i}", dma=True) for i in range(2)]
    NP = 0
    glu_slots = {}
    conv_q = []
    conv_done = [0]

    def drain(n):
        n = min(n, len(conv_q))
        for _ in range(n):
            conv_q.pop(0)()
        conv_done[0] += n

    def glu_group(g):
        while conv_done[0] < 32 * (g - 1):
            drain(1)
        if g not in glu_slots:
            glu_slots[g] = (load_slot(W_A + g), load_slot(W_G + g))
        sa, sg_ = glu_slots[g]
        x = g % 2
        banks = {}
        for hh in range(2):
            for nm, s in (("a", sa), ("g", sg_)):
                bk = next_pb()
                banks[(nm, hh)] = bk

                def fn(e, bk=bk, s=s, hh=hh):
                    for kc in range(32):
                        ins = e.matmul(PBt[bk][:, :], lhsT=SL[s][:, kc * 128:(kc + 1) * 128],
                                       rhs=XT[:, kc, hh * 512:(hh + 1) * 512], start=(kc == 0), stop=(kc == 31))
                    return ins
                P.op("pe", fn, reads=XTq + [SLb[s]], writes=[PBb[bk]])
            ga, gg = banks[("a", hh)], banks[("g", hh)]
            P.op("act", lambda e, gg=gg, hh=hh, x=x, g=g: e.activation(
                out=SG[x][:, 32 + hh * 512:32 + (hh + 1) * 512], in_=PBt[gg][:, :], func=AF.Sigmoid,
                bias=CST[:, C_BGLU + 16 + g:C_BGLU + 17 + g], scale=1.0),
                reads=[PBb[gg]] + KR, writes=[SGb[x]])
            P.op("dve", lambda e, ga=ga, hh=hh, x=x, g=g: e.scalar_tensor_tensor(
                out=CC[x][:, 32 + hh * 512:32 + (hh + 1) * 512], in0=PBt[ga][:, :],
                scalar=CST[:, C_BGLU + g:C_BGLU + g + 1], in1=SG[x][:, 32 + hh * 512:32 + (hh + 1) * 512],
                op0=ALU.add, op1=ALU.mult), reads=[PBb[ga], SGb[x]] + KR, writes=[CCb[x]])
        hb = next_pb()

        def fnh(e, hb=hb, sa=sa, sg_=sg_):
            for nmi, s in enumerate((sa, sg_)):
                for kc in range(32):
                    ins = e.matmul(PBt[hb][:, nmi * 32:(nmi + 1) * 32], lhsT=SL[s][:, kc * 128:(kc + 1) * 128],
                                   rhs=XTH[:, kc, :], start=(kc == 0), stop=(kc == 31))
            return ins
        P.op("pe", fnh, reads=[XTHb, SLb[sa], SLb[sg_]], writes=[PBb[hb]])
        P.op("act", lambda e, hb=hb, x=x, g=g: e.activation(out=SG[x][:, 0:32], in_=PBt[hb][:, 32:64], func=AF.Sigmoid,
                                                            bias=CST[:, C_BGLU + 16 + g:C_BGLU + 17 + g], scale=1.0),
             reads=[PBb[hb]] + KR, writes=[SGb[x]])
        P.op("dve", lambda e, hb=hb, x=x, g=g: e.scalar_tensor_tensor(
            out=CC[x][:, 0:32], in0=PBt[hb][:, 0:32], scalar=CST[:, C_BGLU + g:C_BGLU + g + 1], in1=SG[x][:, 0:32],
            op0=ALU.add, op1=ALU.mult), reads=[PBb[hb], SGb[x]] + KR, writes=[CCb[x]])
        P.op("dve", lambda e, x=x: e.tensor_scalar(CC[x][:, 0:32], CC[x][:, 0:32], CST[:, C_FLAGS:C_FLAGS + 1], None, ALU.mult),
             reads=[CCb[x]] + KR, writes=[CCb[x]])
        cw = lambda j, g=g: CST[:, C_CONVW + g * 31 + j:C_CONVW + g * 31 + j + 1]
        conv_q.append(lambda: P.op("dve", lambda e: e.tensor_scalar(ACC2[x][:, :], CC[x][:, 2:1026], cw(0), CST[:, C_CONVB + g:C_CONVB + g + 1],
                                                            ALU.mult, ALU.add), reads=[CCb[x]] + KR, writes=[ACC2b[x]]))
        for j in range(1, 30):
            conv_q.append(lambda j=j: P.op("dve", lambda e: e.scalar_tensor_tensor(out=ACC2[x][:, :], in0=CC[x][:, 2 + j:1026 + j], scalar=cw(j),
                                                                               in1=ACC2[x][:, :], op0=ALU.mult, op1=ALU.add),
                                           reads=[CCb[x], ACC2b[x]] + KR, writes=[ACC2b[x]]))
        conv_q.append(lambda: P.op("dve", lambda e: e.scalar_tensor_tensor(out=CVS[x][:, :], in0=CC[x][:, 32:1056], scalar=cw(30),
                                                                       in1=ACC2[x][:, :], op0=ALU.mult, op1=ALU.add),
                                   reads=[CCb[x], ACC2b[x]] + KR, writes=[CVSb[x]]))
        conv_q.append(lambda: P.dma("sp", [lambda e: e.dma_start(out=cvd[g, :, :], in_=CVS[x][:, :])], reads=[CVSb[x]], writes=[], owner=CVSb[x]))

    glu_next = [0]

    def glu_after_panel(pi):
        n = [2, 2, 2, 2, 2, 1, 1, 1, 1, 1, 1, 0][pi]
        for _ in range(n):
            glu_group(glu_next[0])
            glu_next[0] += 1

    for sidx, src in ((0, xtp), (1, xto)):
        pre = [load_slot(W_Q + kq) for kq in range(4)] if sidx == 1 else None
        pre_k = load_xt(src, W_K if sidx == 0 else None)
        pi = 0
        if sidx == 1:
            P.dma("pool", [lambda e: e.dma_start(out=XTH[:, :, :], in_=xth.rearrange("(c p) t -> p c t", p=128))],
                  reads=[], writes=[XTHb], owner=XTHb, phase_mem=True)
            for hp in range(4):
                qk_panel(W_Q + 4 * hp, qts, 4 * hp, sidx, 0, slots=pre if hp == 0 else None, qmajor=(hp == 0))
                glu_after_panel(pi)
                pi += 1
        for hp in range(4):
            qk_panel(W_K + 4 * hp, kts, 4 * hp, sidx, sidx * NT, slots=pre_k if (hp == 0 and sidx == 0) else None, qmajor=(hp == 0 and sidx == 0))
            if sidx == 1:
                glu_after_panel(pi)
                pi += 1
        for hp in range(4):
            v_panel(W_V + 4 * hp, 2 * hp, sidx)
            if sidx == 1:
                glu_after_panel(pi)
                pi += 1
    assert glu_next[0] == 16
    drain(len(conv_q))
    if stop_after == "A2":
        return finish(nc, P)
    own_end = P.snapshot()
    P.barrier_partial(("pe",), CVSb)
    cvp = [pa[0] + (pa2_start - pa[0])]
    CV = alloc("cv", [128, 16, NT], BF16, cvp)
    CVb = P.buf("cv", dma=True)
    P.dma("sp", [lambda e, q=q: e.dma_start(out=CV[:, 4 * q:4 * q + 4, :], in_=cvd[4 * q:4 * q + 4, :, :].rearrange("g p t -> p g t"))
                 for q in range(4)], reads=[], writes=[CVb], owner=CVb, extra=own_end)

    pb_ = [phase_base]
    pb2 = [pa[0]]
    QTH = [alloc(f"qth{i}", [128, 2, NT], BF16, pb_) for i in range(2)]
    KTH = [alloc(f"kth{i}", [128, 2, 2 * NT], BF16, pb_) for i in range(2)]
    VH = [alloc(f"vh{i}", [128, 16, 258], BF16, pb_) for i in range(2)]
    QTHb = [P.buf(f"qth{i}", dma=True) for i in range(2)]
    KTHb = [P.buf(f"kth{i}", dma=True) for i in range(2)]
    VHb = [P.buf(f"vh{i}", dma=True) for i in range(2)]
    ET = [alloc(f"et{i}", [128, 2, 2048], BF16, pb_) for i in range(2)]
    ETb = [P.buf(f"et{i}") for i in range(2)]
    assert pb_[0] <= phase_base + 65536
    MST = [alloc(f"mst{i}", [128, 2, NT], BF16, pb2) for i in range(2)]
    MSTb = [P.buf(f"mst{i}", dma=True) for i in range(2)]
    sm = {}
    smb = {}
    for nm, shp, dt in (("rinv", [128, 2], F32), ("r2", [128, 1], F32), ("t1", [128, 256], F32), ("o", [128, 256], F32),
                        ("junk", [128, 256], F32), ("ss", [128, 1], F32), ("rs", [128, 1], F32), ("rr", [128, 1], F32),
                        ("on", [128, 256], BF16)):
        sm[nm] = [alloc(f"{nm}{i}", shp, dt, pb2) for i in range(2)]
        smb[nm] = [P.buf(f"{nm}{i}") for i in range(2)]

    def attn_head(h):
        x = h % 2
        P.dma("pool", [lambda e, h=h, x=x: e.dma_start(out=QTH[x][:, :, :], in_=qts[2 * h:2 * h + 2, :, :].rearrange("c d t -> d c t"))],
              reads=[], writes=[QTHb[x]], owner=QTHb[x], phase_mem=True)
        P.dma("pool", [lambda e, h=h, x=x: e.dma_start(out=KTH[x][:, :, :], in_=kts[2 * h:2 * h + 2, :, :].rearrange("c d t -> d c t"))],
              reads=[], writes=[KTHb[x]], owner=KTHb[x], phase_mem=True)
        P.dma("pool", [lambda e, h=h, x=x: e.dma_start(out=VH[x][:, :, 0:257], in_=vsd[h, :, :, 0:257].rearrange("t p c -> p t c"))],
              reads=[], writes=[VHb[x]], owner=VHb[x], phase_mem=True)
        z = h % 2
        obs = {}

        def QK(i):
            y = i % 2
            nj = 9 + i
            for m in range(2):
                for jb in range((nj + 3) // 4):
                    bk = next_pb()
                    js = list(range(jb * 4, min(nj, jb * 4 + 4)))

                    def fn(e, bk=bk, js=js, m=m, i=i):
                        for j in js:
                            ins = e.matmul(PBt[bk][:, (j % 4) * 128:(j % 4 + 1) * 128], lhsT=KTH[x][:, m, j * 128:(j + 1) * 128],
                                           rhs=QTH[x][:, m, i * 128:(i + 1) * 128], start=True, stop=True)
                        return ins
                    P.op("pe", fn, reads=[KTHb[x], QTHb[x]], writes=[PBb[bk]])
                    P.op("act", lambda e, bk=bk, js=js, m=m, y=y: e.activation(
                        out=ET[y][:, m, js[0] * 128:(js[-1] + 1) * 128], in_=PBt[bk][:, 0:len(js) * 128], func=AF.Exp, scale=SM_SCALE),
                        reads=[PBb[bk]], writes=[ETb[y]])
                jd = 8 + i
                P.op("dve", lambda e, m=m, y=y, jd=jd: e.tensor_tensor(out=ET[y][:, m, jd * 128:(jd + 1) * 128],
                                                                       in0=ET[y][:, m, jd * 128:(jd + 1) * 128], in1=TRI[:, :], op=ALU.mult),
                     reads=[ETb[y]] + KR, writes=[ETb[y]])

        def PV(i):
            y = i % 2
            nj = 9 + i
            ob = []
            for m in range(2):
                bk = next_pb()
                ob.append(bk)

                def fn(e, bk=bk, m=m, nj=nj, y=y):
                    for j in range(nj):
                        ins = e.matmul(PBt[bk][:, 0:257], lhsT=ET[y][:, m, j * 128:(j + 1) * 128], rhs=VH[x][:, j, 0:257],
                                       start=(j == 0), stop=(j == nj - 1))
                    return ins
                P.op("pe", fn, reads=[ETb[y], VHb[x]], writes=[PBb[bk]])
            T = {k: v[y] for k, v in sm.items()}
            Tb = {k: v[y] for k, v in smb.items()}
            for m in range(2):
                P.op("dve", lambda e, m=m, bk=ob[m]: e.reciprocal(out=T["rinv"][:, m:m + 1], in_=PBt[bk][:, 256:257]),
                     reads=[PBb[ob[m]]], writes=[Tb["rinv"]])
            P.op("dve", lambda e: e.tensor_tensor(out=T["r2"][:, :], in0=T["rinv"][:, 1:2], in1=NLAM[:, :], op=ALU.mult),
                 reads=[Tb["rinv"]] + KR, writes=[Tb["r2"]])
            P.op("dve", lambda e: e.tensor_scalar(T["t1"][:, :], PBt[ob[0]][:, 0:256], T["rinv"][:, 0:1], None, ALU.mult),
                 reads=[PBb[ob[0]], Tb["rinv"]], writes=[Tb["t1"]])
            P.op("dve", lambda e: e.scalar_tensor_tensor(out=T["o"][:, :], in0=PBt[ob[1]][:, 0:256], scalar=T["r2"][:, 0:1],
                                                         in1=T["t1"][:, :], op0=ALU.mult, op1=ALU.add),
                 reads=[PBb[ob[1]], Tb["r2"], Tb["t1"]], writes=[Tb["o"]])
            P.op("dve", lambda e: e.scalar_tensor_tensor(out=T["junk"][:, :], in0=T["o"][:, :], scalar=1.0, in1=T["o"][:, :],
                                                         op0=ALU.mult, op1=ALU.mult, accum_out=T["ss"][:, 0:1]),
                 reads=[Tb["o"]], writes=[Tb["junk"], Tb["ss"]])

        def POST2(i):
            y = i % 2
            T = {k: v[y] for k, v in sm.items()}
            Tb = {k: v[y] for k, v in smb.items()}
            P.op("act", lambda e: e.activation(out=T["rs"][:, :], in_=T["ss"][:, :], func=AF.Ln, scale=1.0 / 256.0, bias=EPSC[:, 0:1]),
                 reads=[Tb["ss"]] + KR, writes=[Tb["rs"]])
            P.op("act", lambda e: e.activation(out=T["rr"][:, :], in_=T["rs"][:, :], func=AF.Exp, scale=-0.5),
                 reads=[Tb["rs"]], writes=[Tb["rr"]])
            P.op("dve", lambda e: e.scalar_tensor_tensor(out=T["on"][:, :], in0=T["o"][:, :], scalar=T["rr"][:, 0:1], in1=G8[:, :],
                                                         op0=ALU.mult, op1=ALU.mult), reads=[Tb["o"], Tb["rr"]] + KR, writes=[Tb["on"]])

        def TR(i):
            y = i % 2
            tb = next_tb()

            def fn(e):
                for c in range(2):
                    ins = e.transpose(out=TBt[tb][:, c * 128:(c + 1) * 128], in_=sm["on"][y][:, c * 128:(c + 1) * 128], identity=IDENT[:, :])
                return ins
            P.op("pe", fn, reads=[smb["on"][y]] + KR, writes=[TBb[tb]])
            P.op("act", lambda e: e.activation(out=MST[z][:, :, i * 128:(i + 1) * 128],
                                               in_=TBt[tb][:, 0:256].rearrange("p (c t) -> p c t", c=2), func=AF.Copy),
                 reads=[TBb[tb]], writes=[MSTb[z]])

        for i in range(11):
            if i < 8:
                QK(i)
            if 1 <= i <= 8:
                PV(i - 1)
            if 2 <= i <= 9:
                POST2(i - 2)
            if i >= 3:
                TR(i - 3)
        P.dma("sp", [lambda e, h=h, z=z: e.dma_start(out=mts[2 * h:2 * h + 2, :, :].rearrange("c d t -> d c t"), in_=MST[z][:, :, :])],
              reads=[MSTb[z]], writes=[], owner=MSTb[z])

    assert pb2[0] <= pa1[0], (pb2[0], pa1[0])
    for h in range(8):
        attn_head(h)
    P.barrier()
    if stop_after == "B":
        return finish(nc, P)

    pd = [phase_base]
    MH = alloc("mh", [128, 32, 512], BF16, pd)
    MHb = P.buf("mh", dma=True)
    MHc = [P.buf(f"mhc{k}") for k in range(32)]
    P.dma("sp", [lambda e, q=q: e.dma_start(out=MH[:, 8 * q:8 * q + 8, :], in_=mts[8 * q:8 * q + 8, :, 0:512].rearrange("c p t -> p c t"))
                 for q in range(2)], reads=[], writes=MHc[0:16], owner=MHb)
    pc_ = [pd[0]]
    assert pc_[0] + 49152 <= pa2_start
    SQ = alloc("sq", [128, 16, NT], BF16, pc_)
    SQbs = [P.buf(f"sq{g}") for g in range(16)]
    TMP = alloc("tmpb", [128, NT], F32, pc_)
    STb = P.buf("stats")
    YT = [alloc(f"yt{i}", [128, NT], F32, pc_) for i in range(2)]
    YTb = [P.buf(f"yt{i}") for i in range(2)]
    MS = [alloc(f"ms{i}", [128, NT], BF16, pc_) for i in range(2)]
    MSb = [P.buf(f"ms{i}", dma=True) for i in range(2)]
    for g in range(16):
        if g % 2 == 0:
            P.op("act", lambda e, g=g: e.activation(out=SQ[:, g, :], in_=CV[:, g, :], func=AF.Square), reads=[CVb], writes=[SQbs[g]])
        else:
            P.op("dve", lambda e, g=g: e.tensor_tensor(out=SQ[:, g, :], in0=CV[:, g, :], in1=CV[:, g, :], op=ALU.mult), reads=[CVb], writes=[SQbs[g]])
    sbk = {}
    for nm, SRC, srcb in (("s", CV, [CVb]), ("q", SQ, SQbs)):
        for hh in range(2):
            bk = next_pb()
            sbk[(nm, hh)] = bk

            def fn(e, bk=bk, SRC=SRC, hh=hh):
                for g in range(16):
                    ins = e.matmul(PBt[bk][:, :], lhsT=ONESC[:, :], rhs=SRC[:, g, hh * 512:(hh + 1) * 512], start=(g == 0), stop=(g == 15))
                return ins
            P.op("pe", fn, reads=srcb + KR, writes=[PBb[bk]])
    for hh in range(2):
        sl = slice(hh * 512, (hh + 1) * 512)
        bs, bq = sbk[("s", hh)], sbk[("q", hh)]
        P.op("act", lambda e, sl=sl, bs=bs: e.activation(out=TMP[:, sl], in_=PBt[bs][:, :], func=AF.Square), reads=[PBb[bs]], writes=[STb])
        P.op("dve", lambda e, sl=sl, bq=bq: e.tensor_tensor(out=TMP[:, sl], in0=PBt[bq][:, :], in1=TMP[:, sl], op=ALU.subtract), reads=[PBb[bq], STb], writes=[STb])
        P.op("act", lambda e, sl=sl: e.activation(out=TMP[:, sl], in_=TMP[:, sl], func=AF.Sqrt, bias=EPSC[:, 0:1], scale=1.0), reads=[STb] + KR, writes=[STb])
        P.op("dve", lambda e, sl=sl, bq=bq: e.reciprocal(out=PBt[bq][:, :], in_=TMP[:, sl]), reads=[STb], writes=[PBb[bq]])
    for g in range(16):
        x = g % 2
        for hh in range(2):
            sl = slice(hh * 512, (hh + 1) * 512)
            bs, bq = sbk[("s", hh)], sbk[("q", hh)]
            P.op("dve", lambda e, g=g, x=x, sl=sl, bs=bs: e.tensor_tensor(out=YT[x][:, sl], in0=CV[:, g, sl], in1=PBt[bs][:, :], op=ALU.subtract),
                 reads=[CVb, PBb[bs]], writes=[YTb[x]])
            P.op("dve", lambda e, x=x, sl=sl, bq=bq: e.tensor_tensor(out=YT[x][:, sl], in0=YT[x][:, sl], in1=PBt[bq][:, :], op=ALU.mult),
                 reads=[YTb[x], PBb[bq]], writes=[YTb[x]])
        P.op("act", lambda e, g=g, x=x: e.activation(out=MH[:, 16 + g, :], in_=YT[x][:, 0:512], func=AF.Silu,
                                                     scale=CST[:, C_CLNG + g:C_CLNG + g + 1], bias=CST[:, C_CLNB + g:C_CLNB + g + 1]),
             reads=[YTb[x]] + KR, writes=[MHc[16 + g]])
        P.op("act", lambda e, g=g, x=x: e.activation(out=MS[x][:, 0:512], in_=YT[x][:, 512:1024], func=AF.Silu,
                                                     scale=CST[:, C_CLNG + g:C_CLNG + g + 1], bias=CST[:, C_CLNB + g:C_CLNB + g + 1]),
             reads=[YTb[x]] + KR, writes=[MSb[x]])
        P.dma("sp", [lambda e, g=g, x=x: e.dma_start(out=mts[16 + g, :, 512:1024], in_=MS[x][:, 0:512])], reads=[MSb[x]], writes=[], owner=MSb[x])
    P.barrier_targets(P.snapshot(), ("act", "dve", "sp"))
    if stop_after == "B2":
        return finish(nc, P)

    ACC = alloc("acc", [128, 32, 512], F32, pd)
    ACCb = [P.buf(f"acc{i}") for i in range(32)]
    OUTb = P.buf("outdma", dma=True)
    HID = [alloc(f"hid{i}", [128, FBG, 512], BF16, pd) for i in range(2)]
    HIDb = [P.buf(f"hid{i}") for i in range(2)]
    XRES = [alloc(f"xres{i}", [128, 512], F32, pd) for i in range(2)]
    XRESb = [P.buf(f"xres{i}", dma=True) for i in range(2)]
    LTMP = alloc("ltmp", [128, 512], F32, pd)
    SUMV = alloc("sumv", [128, 512], F32, pd)
    SQV = alloc("sqv", [128, 512], F32, pd)
    SUMVb = P.buf("sumv")
    SQVb = P.buf("sqv")
    OST = [alloc(f"ost{i}", [128, 512], F32, pd) for i in range(2)]
    OSTb = [P.buf(f"ost{i}", dma=True) for i in range(2)]
    LSTb = P.buf("lstats")
    SQC = [alloc(f"sqc{i}", [128, 512], F32, pd) for i in range(2)]
    SQCb = [P.buf(f"sqc{i}") for i in range(2)]
    SGC = [alloc(f"sgc{i}", [128, 512], F32, pd) for i in range(2)]
    SGCb = [P.buf(f"sgc{i}") for i in range(2)]
    xtov = xto.rearrange("(c p) t -> p c t", p=128)
    ytv = yt.rearrange("(c p) t -> p c t", p=128)

    def ln_finish(bs, bq):
        P.op("act", lambda e: e.activation(out=LTMP[:, :], in_=PBt[bs][:, :], func=AF.Square), reads=[PBb[bs]], writes=[LSTb])
        P.op("dve", lambda e: e.tensor_tensor(out=LTMP[:, :], in0=PBt[bq][:, :], in1=LTMP[:, :], op=ALU.subtract), reads=[PBb[bq], LSTb], writes=[LSTb])
        P.op("act", lambda e: e.activation(out=LTMP[:, :], in_=LTMP[:, :], func=AF.Sqrt, bias=EPSC[:, 0:1], scale=1.0), reads=[LSTb] + KR, writes=[LSTb])
        P.op("dve", lambda e: e.reciprocal(out=PBt[bq][:, :], in_=LTMP[:, :]), reads=[LSTb], writes=[PBb[bq]])

    def ln_stats_mm():
        bs, bq = next_pb(), next_pb()
        for dc in range(32):
            x = dc % 2
            P.op("act", lambda e, dc=dc, x=x: e.activation(out=SQC[x][:, :], in_=ACC[:, dc, :], func=AF.Square), reads=[ACCb[dc]], writes=[SQCb[x]])
            P.op("pe", lambda e, dc=dc: e.matmul(PBt[bs][:, :], lhsT=ONESF[:, :], rhs=ACC[:, dc, :], start=(dc == 0), stop=(dc == 31)),
                 reads=[ACCb[dc]] + KR, writes=[PBb[bs]])
            P.op("pe", lambda e, dc=dc, x=x: e.matmul(PBt[bq][:, :], lhsT=ONESF[:, :], rhs=SQC[x][:, :], start=(dc == 0), stop=(dc == 31)),
                 reads=[SQCb[x]] + KR, writes=[PBb[bq]])
        ln_finish(bs, bq)
        return bs, bq

    def ln_accum(dc):
        x = dc % 2
        if dc == 0:
            P.op("dve", lambda e: e.tensor_copy(out=SUMV[:, :], in_=ACC[:, 0, :]), reads=[ACCb[0]], writes=[SUMVb])
            P.op("act", lambda e: e.activation(out=SQV[:, :], in_=ACC[:, 0, :], func=AF.Square), reads=[ACCb[0]], writes=[SQVb])
        else:
            P.op("dve", lambda e: e.tensor_tensor(out=SUMV[:, :], in0=SUMV[:, :], in1=ACC[:, dc, :], op=ALU.add), reads=[ACCb[dc], SUMVb], writes=[SUMVb])
            P.op("act", lambda e: e.activation(out=SQC[x][:, :], in_=ACC[:, dc, :], func=AF.Square), reads=[ACCb[dc]], writes=[SQCb[x]])
            P.op("dve", lambda e: e.tensor_tensor(out=SQV[:, :], in0=SQV[:, :], in1=SQC[x][:, :], op=ALU.add), reads=[SQCb[x], SQVb], writes=[SQVb])

    def ln_stats_acc():
        bs, bq = next_pb(), next_pb()
        P.op("pe", lambda e: e.matmul(PBt[bs][:, :], lhsT=ONESF[:, :], rhs=SUMV[:, :], start=True, stop=True), reads=[SUMVb] + KR, writes=[PBb[bs]])
        P.op("pe", lambda e: e.matmul(PBt[bq][:, :], lhsT=ONESF[:, :], rhs=SQV[:, :], start=True, stop=True), reads=[SQVb] + KR, writes=[PBb[bq]])
        ln_finish(bs, bq)
        return bs, bq

    def ln_norm(dc, bs, bq):
        P.op("dve", lambda e: e.tensor_tensor(out=ACC[:, dc, :], in0=ACC[:, dc, :], in1=PBt[bs][:, :], op=ALU.subtract), reads=[ACCb[dc], PBb[bs]], writes=[ACCb[dc]])
        P.op("dve", lambda e: e.tensor_tensor(out=ACC[:, dc, :], in0=ACC[:, dc, :], in1=PBt[bq][:, :], op=ALU.mult), reads=[ACCb[dc], PBb[bq]], writes=[ACCb[dc]])

    def mh_load(tsl):
        P.dma("sp", [lambda e, q=q, tsl=tsl: e.dma_start(out=MH[:, 8 * q:8 * q + 8, :], in_=mts[8 * q:8 * q + 8, :, tsl].rearrange("c p t -> p c t"))
                     for q in range(4)], reads=[], writes=MHc, owner=MHb)

    def c1_chunk(dc, tsl):
        s = load_slot(W_O + dc)
        x = dc % 2
        P.dma("sp", [lambda e: e.dma_start(out=XRES[x][:, :], in_=xtov[:, dc, tsl])], reads=[], writes=[XRESb[x]], owner=XRESb[x])
        bk = next_pb()

        def fn(e):
            for kc in range(32):
                ins = e.matmul(PBt[bk][:, :], lhsT=SL[s][:, kc * 128:(kc + 1) * 128], rhs=MH[:, kc, :], start=(kc == 0), stop=(kc == 31))
            return ins
        P.op("pe", fn, reads=MHc + [SLb[s]], writes=[PBb[bk]])
        P.op("dve", lambda e: e.scalar_tensor_tensor(out=ACC[:, dc, :], in0=XRES[x][:, :], scalar=ALPHA, in1=PBt[bk][:, :],
                                                     op0=ALU.mult, op1=ALU.add), reads=[XRESb[x], PBb[bk]], writes=[ACCb[dc]])
        ln_accum(dc)

    def ln2_chunk(dc, bs, bq, tsl):
        x = dc % 2
        ln_norm(dc, bs, bq)
        P.op("act", lambda e: e.activation(out=OST[x][:, :], in_=ACC[:, dc, :], func=AF.Identity,
                                           scale=CST[:, C_LN2G + dc:C_LN2G + dc + 1], bias=CST[:, C_LN2B + dc:C_LN2B + dc + 1]),
             reads=[ACCb[dc]] + KR, writes=[OSTb[x]])
        P.dma("act", [lambda e: e.dma_start(out=ytv[:, dc, tsl], in_=OST[x][:, :])], reads=[OSTb[x]], writes=[], owner=OSTb[x])

    for grp in range(2):
        tsl = slice(grp * 512, (grp + 1) * 512)
        if grp == 0:
            for dc in range(32):
                c1_chunk(dc, tsl)
        bs, bq = ln_stats_acc()
        for dc in range(32):
            ln_norm(dc, bs, bq)
            P.op("act", lambda e, dc=dc: e.activation(out=MH[:, dc, :], in_=ACC[:, dc, :], func=AF.Identity,
                                                      scale=CST[:, C_LN1G + dc:C_LN1G + dc + 1], bias=CST[:, C_LN1B + dc:C_LN1B + dc + 1]),
                 reads=[ACCb[dc]] + KR, writes=[MHc[dc]])
            P.op("act", lambda e, dc=dc: e.activation(out=ACC[:, dc, :], in_=ACC[:, dc, :], func=AF.Identity,
                                                      scale=AG1[:, dc:dc + 1], bias=AB1[:, dc:dc + 1]),
                 reads=[ACCb[dc]] + KR, writes=[ACCb[dc]])
        sbs = [list(range(i, min(i + FBG, NFB))) for i in range(0, NFB, FBG)]

        def GU(si):
            hx = si % 2
            fbs = sbs[si]
            fi0 = 0
            if si == 0:
                sl4, bk4 = [], []
                for fb in fbs[:2]:
                    for w in (W_GT, W_UP):
                        sl4.append(load_slot(w + fb))
                        bk4.append(next_pb())
                for kc in range(32):
                    for s, bk in zip(sl4, bk4):
                        P.op("pe", lambda e, s=s, bk=bk, kc=kc: e.matmul(PBt[bk][:, :], lhsT=SL[s][:, kc * 128:(kc + 1) * 128], rhs=MH[:, kc, :],
                                                                         start=(kc == 0), stop=(kc == 31)),
                             reads=[MHc[kc], SLb[s]], writes=[PBb[bk]])
                for fi, fb in enumerate(fbs[:2]):
                    x = fb % 2
                    bg, bu = bk4[2 * fi], bk4[2 * fi + 1]
                    P.op("act", lambda e, bk=bg, x=x: e.activation(out=SGC[x][:, :], in_=PBt[bk][:, :], func=AF.Silu), reads=[PBb[bg]], writes=[SGCb[x]])
                    P.op("dve", lambda e, bk=bu, x=x, fi=fi, hx=hx: e.tensor_tensor(out=HID[hx][:, fi, :], in0=SGC[x][:, :], in1=PBt[bk][:, :], op=ALU.mult),
                         reads=[SGCb[x], PBb[bu]], writes=[HIDb[hx]])
                fi0 = 2
            for fi, fb in list(enumerate(fbs))[fi0:]:
                s_g = load_slot(W_GT + fb)
                s_u = load_slot(W_UP + fb)
                bks = []
                for s in (s_g, s_u):
                    bk = next_pb()
                    bks.append(bk)

                    def fn(e, bk=bk, s=s):
                        for kc in range(32):
                            ins = e.matmul(PBt[bk][:, :], lhsT=SL[s][:, kc * 128:(kc + 1) * 128], rhs=MH[:, kc, :], start=(kc == 0), stop=(kc == 31))
                        return ins
                    P.op("pe", fn, reads=MHc + [SLb[s]], writes=[PBb[bk]])
                x = fb % 2
                P.op("act", lambda e, bk=bks[0], x=x: e.activation(out=SGC[x][:, :], in_=PBt[bk][:, :], func=AF.Silu), reads=[PBb[bks[0]]], writes=[SGCb[x]])
                P.op("dve", lambda e, bk=bks[1], x=x, fi=fi, hx=hx: e.tensor_tensor(out=HID[hx][:, fi, :], in0=SGC[x][:, :], in1=PBt[bk][:, :], op=ALU.mult),
                     reads=[SGCb[x], PBb[bks[1]]], writes=[HIDb[hx]])

        def DOWN(si):
            hx = si % 2
            fbs = sbs[si]
            slots = [load_slot(W_DW + fb) for fb in fbs]
            for dc in range(32):
                bk = next_pb()

                def fn(e, bk=bk, dc=dc):
                    for fi in range(len(fbs)):
                        ins = e.matmul(PBt[bk][:, :], lhsT=SL[slots[fi]][:, dc * 128:(dc + 1) * 128], rhs=HID[hx][:, fi, :],
                                       start=(fi == 0), stop=(fi == len(fbs) - 1))
                    return ins
                P.op("pe", fn, reads=[HIDb[hx]] + [SLb[s] for s in slots], writes=[PBb[bk]])
                P.op("dve", lambda e, bk=bk, dc=dc: e.tensor_tensor(out=ACC[:, dc, :], in0=ACC[:, dc, :], in1=PBt[bk][:, :], op=ALU.add),
                     reads=[ACCb[dc], PBb[bk]], writes=[ACCb[dc]])

        for si in range(len(sbs) + 1):
            if si < len(sbs):
                GU(si)
            if si >= 1:
                DOWN(si - 1)
        bs, bq = ln_stats_mm()
        if grp == 0:
            tsl1 = slice(512, 1024)
            mh_load(tsl1)
            pinned.update((bs, bq))
            for dc in range(32):
                ln2_chunk(dc, bs, bq, tsl)
                c1_chunk(dc, tsl1)
            pinned.clear()
        else:
            for dc in range(32):
                ln2_chunk(dc, bs, bq, tsl)
    P.barrier()
    return finish(nc, P)


def finish(nc, P):
    P.barrier(pool=True)
    with nc.Block() as block:
        @block.tensor
        def _(e):
            P.replay("pe", e)

        @block.scalar
        def _(e):
            P.replay("act", e)

        @block.vector
        def _(e):
            P.replay("dve", e)

        @block.gpsimd
        def _(e):
            P.replay("pool", e)

        @block.sync
        def _(e):
            P.replay("sp", e)
    return nc


def build_wstream(w_in, w_o, w_gate, w_up, w_down):
    wst = np.empty((NSLOT, 128, 4096), dtype=np.float32)

    def put_x(base, W, col0):
        Pn = W[:, col0:col0 + 512].reshape(32, 128, 512)
        for kq in range(4):
            wst[base + kq] = Pn[kq * 8:(kq + 1) * 8].transpose(1, 0, 2).reshape(128, 4096)

    def put_s(idx, W, col0):
        wst[idx] = W[:, col0:col0 + 128].reshape(32, 128, 128).transpose(1, 0, 2).reshape(128, 4096)

    for hp in range(4):
        put_x(W_Q + 4 * hp, w_in, 512 * hp)
        put_x(W_K + 4 * hp, w_in, 2048 + 512 * hp)
        put_x(W_V + 4 * hp, w_in, 4096 + 512 * hp)
    for g in range(16):
        put_s(W_A + g, w_in, 6144 + 128 * g)
        put_s(W_G + g, w_in, 6144 + 2048 + 128 * g)
    for dc in range(32):
        put_s(W_O + dc, w_o, 128 * dc)
    for fb in range(NFB):
        put_s(W_GT + fb, w_gate, 128 * fb)
        put_s(W_UP + fb, w_up, 128 * fb)
        wst[W_DW + fb] = w_down[fb * 128:(fb + 1) * 128, :]
    return wst.reshape(NSLOT * 128, 4096)


def col32(v):
    return np.ascontiguousarray(v.reshape(-1, 128).T)


def prep_inputs(inp):
    x = np.asarray(inp["x"], dtype=np.float32)
    pos = np.asarray(inp["positions"]).astype(np.int32)
    wst = build_wstream(np.asarray(inp["w_in"][0]), np.asarray(inp["w_o"][0]), np.asarray(inp["w_gate"][0]),
                        np.asarray(inp["w_up"][0]), np.asarray(inp["w_down"][0]))
    cst0 = np.zeros((128, NCST), dtype=np.float32)
    cst0[:, C_IDENT:C_IDENT + 128] = np.eye(128, dtype=np.float32)
    cst0[:, C_TRI:C_TRI + 128] = np.triu(np.ones((128, 128), dtype=np.float32))
    cst0[:, C_BGLU:C_BGLU + 32] = col32(np.asarray(inp["b_glu"][0]))
    cw = np.asarray(inp["conv_w"][0])
    cst0[:, C_CONVW:C_CONVW + 496] = cw.reshape(31, 16, 128).transpose(2, 1, 0).reshape(128, 496)
    cst0[:, C_CONVB:C_CONVB + 16] = col32(np.asarray(inp["conv_b"][0]))
    cst0[:, C_CLNG:C_CLNG + 16] = col32(np.asarray(inp["conv_ln_g"][0]))
    cst0[:, C_CLNB:C_CLNB + 16] = col32(np.asarray(inp["conv_ln_b"][0]))
    cst0[:, C_LN1G:C_LN1G + 32] = col32(np.asarray(inp["ln1_g"][0]))
    cst0[:, C_LN1B:C_LN1B + 32] = col32(np.asarray(inp["ln1_b"][0]))
    cst0[:, C_LN2G:C_LN2G + 32] = col32(np.asarray(inp["ln2_g"][0]))
    cst0[:, C_LN2B:C_LN2B + 32] = col32(np.asarray(inp["ln2_b"][0]))
    for k, nm in enumerate(("lam_q1", "lam_k1", "lam_q2", "lam_k2")):
        cst0[:, C_LAMV + 128 * k:C_LAMV + 128 * (k + 1)] = np.asarray(inp[nm][0])[None, :]
    cst0[:, C_SUBG:C_SUBG + 256] = np.asarray(inp["subln_g"][0])[None, :]
    cst0[:, C_INVF:C_INVF + 16] = (500000.0 ** (-np.arange(0, 32, 2, dtype=np.float32) / 32.0)).astype(np.float32)[None, :]
    in_maps = []
    for c in range(8):
        b, hf = c // 2, c % 2
        own = x[b, hf * NT:(hf + 1) * NT, :]
        xto = np.ascontiguousarray(own.T)
        if hf == 1:
            xtp = np.ascontiguousarray(x[b, 0:NT, :].T)
        else:
            xtp = np.zeros((D, NT), dtype=np.float32)
        xth = np.ascontiguousarray(xtp[:, NT - 32:])
        posi = np.empty((128, 16), dtype=np.int32)
        pprev = pos[b, 0:NT] if hf == 1 else pos[b, 0:NT]
        posi[:, 0:8] = pprev.reshape(8, 128).T
        posi[:, 8:16] = pos[b, hf * NT:(hf + 1) * NT].reshape(8, 128).T
        cst = cst0.copy()
        cst[:, C_FLAGS] = float(hf)
        cst[:, C_FLAGS + 1] = 1.0
        in_maps.append({"wst": wst, "xto": xto, "xtp": xtp, "xth": xth, "posi": posi, "cst": cst})
    return in_maps


def kernel(**inputs):
    in_maps = prep_inputs(inputs)
    nc = build_program()
    res = run_bass_kernel_spmd(nc, in_maps, core_ids=list(range(8)))
    out = np.empty((B, S, D), dtype=np.float32)
    for c in range(8):
        b, hf = c // 2, c % 2
        out[b, hf * NT:(hf + 1) * NT, :] = np.asarray(res.results[c]["yt"]).T
    return out
```

```python
import math
import numpy as np
import concourse.bass as bass
import concourse.mybir as mybir
from concourse.bass_utils import run_bass_kernel_spmd

F32 = mybir.dt.float32
BF16 = mybir.dt.bfloat16
I32 = mybir.dt.int32
AF = mybir.ActivationFunctionType
ALU = mybir.AluOpType
AX = mybir.AxisListType

D = 4096
S = 2048
B = 4
NT = 1024
DFF = 11008
NFB = DFF // 128
NS = 8
FBG = 4
ALPHA = 2.0 ** 0.25
LAMBDA_INIT = 0.8 - 0.6 * math.exp(0.0)
EPS = 1e-5
SM_SCALE = 128.0 ** -0.5

W_Q = 0
W_K = 16
W_V = 32
W_A = 48
W_G = 64
W_O = 80
W_GT = 112
W_UP = 112 + NFB
W_DW = 112 + 2 * NFB
NSLOT = 112 + 3 * NFB

C_IDENT = 0
C_TRI = 128
C_BGLU = 256
C_CONVW = C_BGLU + 32
C_CONVB = C_CONVW + 16 * 31
C_CLNG = C_CONVB + 16
C_CLNB = C_CLNG + 16
C_LN1G = C_CLNB + 16
C_LN1B = C_LN1G + 32
C_LN2G = C_LN1B + 32
C_LN2B = C_LN2G + 32
C_FLAGS = C_LN2B + 32
C_LAMV = C_FLAGS + 2
C_SUBG = C_LAMV + 512
C_INVF = C_SUBG + 256
NCST = C_INVF + 16


class Buf:
    __slots__ = ("name", "writer", "readers", "sem", "cnt")

    def __init__(self, name):
        self.name = name
        self.writer = None
        self.readers = []
        self.sem = None
        self.cnt = 0


class Prog:
    def __init__(self, nc):
        self.nc = nc
        self.eng = ["pe", "act", "dve", "pool", "sp"]
        self.q = {e: [] for e in self.eng}
        self.esem = {e: nc.alloc_semaphore(name="s_" + e) for e in ("pe", "act", "dve", "pool")}
        self.ecnt = {e: 0 for e in ("pe", "act", "dve", "pool")}
        self.waited = {e: {} for e in self.eng}
        self.pending = {e: [] for e in self.eng}
        self.dmabufs = []
        self.nsem = 3
        self.pool_bar = []

    def buf(self, name, dma=False):
        b = Buf(name)
        if dma:
            b.sem = self.nc.alloc_semaphore(name="d_" + name)
            self.nsem += 1
            self.dmabufs.append(b)
        return b

    def _waits(self, eng, toks):
        out = []
        toks = list(toks) + self.pending[eng]
        self.pending[eng] = []
        w = self.waited[eng]
        best = {}
        for t in toks:
            if t is None:
                continue
            key, sem, val, src = t
            if src == eng and eng == "pe":
                continue
            if w.get(key, 0) >= val:
                continue
            if key not in best or best[key][1] < val:
                best[key] = (sem, val)
        for key, (sem, val) in best.items():
            w[key] = val
            out.append((sem, val))
        return out

    def _deps(self, reads, writes):
        toks = []
        for b in reads:
            toks.append(b.writer)
        for b in writes:
            toks.append(b.writer)
            toks.extend(b.readers)
        return toks

    def _update(self, tok, reads, writes):
        for b in reads:
            b.readers.append(tok)
        for b in writes:
            b.writer = tok
            b.readers = []

    def op(self, eng, fn, reads=(), writes=()):
        toks = self._deps(reads, writes)
        waits = self._waits(eng, toks)
        self.ecnt[eng] += 1
        tok = ("e_" + eng, self.esem[eng], self.ecnt[eng], eng)
        self.q[eng].append((waits, [fn], (self.esem[eng], 1)))
        self._update(tok, reads, writes)

    def snapshot(self):
        toks = []
        for e in ("pe", "act", "dve", "pool"):
            if self.ecnt[e]:
                toks.append(("e_" + e, self.esem[e], self.ecnt[e], "bar"))
        for b in self.dmabufs:
            if b.cnt:
                toks.append(("d_" + b.name, b.sem, b.cnt, "bar"))
        return toks

    def dma(self, queue, fns, reads, writes, owner, phase_mem=False, extra=()):
        toks = self._deps(reads, writes) + list(extra)
        if phase_mem:
            toks = toks + self.pool_bar
        waits = self._waits(queue, toks)
        owner.cnt += 16 * len(fns)
        tok = ("d_" + owner.name, owner.sem, owner.cnt, "dma")
        self.q[queue].append((waits, list(fns), (owner.sem, 16)))
        self._update(tok, reads, writes)

    def barrier(self, pool=False):
        toks = []
        for e in ("pe", "act", "dve", "pool"):
            if self.ecnt[e]:
                toks.append(("e_" + e, self.esem[e], self.ecnt[e], "bar"))
        for b in self.dmabufs:
            if b.cnt:
                toks.append(("d_" + b.name, b.sem, b.cnt, "bar"))
        for e in self.eng:
            if e == "pool" and not pool:
                continue
            self.pending[e] = list(toks)
        self.pool_bar = list(toks)

    def barrier_partial(self, engines, skip_bufs):
        toks = []
        for e in engines:
            if self.ecnt[e]:
                toks.append(("e_" + e, self.esem[e], self.ecnt[e], "bar"))
        for b in self.dmabufs:
            if b.cnt and b not in skip_bufs:
                toks.append(("d_" + b.name, b.sem, b.cnt, "bar"))
        for e in self.eng:
            if e == "pool":
                continue
            self.pending[e] = self.pending[e] + list(toks)
        self.pool_bar = self.pool_bar + list(toks)

    def barrier_targets(self, toks, targets):
        for e in targets:
            self.pending[e] = self.pending[e] + list(toks)

    def replay(self, name, e):
        for waits, fns, inc in self.q[name]:
            for sem, val in waits:
                e.wait_ge(sem, val)
            for fn in fns:
                ins = fn(e)
                ins.then_inc(inc[0], inc[1])
        for sem, val in self._waits(name, []):
            e.wait_ge(sem, val)


def build_program(debug=False, stop_after=None):
    nc = bass.Bass("TRN2", target_bir_lowering=False)
    P = Prog(nc)
    skind = "ExternalOutput" if debug else "Internal"

    wst = nc.dram_tensor("wst", [NSLOT * 128, 4096], F32, kind="ExternalInput").ap()
    xto = nc.dram_tensor("xto", [D, NT], F32, kind="ExternalInput").ap()
    xtp = nc.dram_tensor("xtp", [D, NT], F32, kind="ExternalInput").ap()
    xth = nc.dram_tensor("xth", [D, 32], F32, kind="ExternalInput").ap()
    posi = nc.dram_tensor("posi", [128, 16], I32, kind="ExternalInput").ap()
    cst = nc.dram_tensor("cst", [128, NCST], F32, kind="ExternalInput").ap()
    yt = nc.dram_tensor("yt", [D, NT], F32, kind="ExternalOutput").ap()
    qts = nc.dram_tensor("qts", [16, 128, NT], BF16, kind=skind).ap()
    kts = nc.dram_tensor("kts", [16, 128, 2 * NT], BF16, kind=skind).ap()
    vsd = nc.dram_tensor("vsd", [8, 16, 128, 258], BF16, kind=skind).ap()
    cvd = nc.dram_tensor("cvd", [16, 128, NT], BF16, kind=skind).ap()
    mts = nc.dram_tensor("mts", [32, 128, NT], BF16, kind=skind).ap()

    base = (nc.sbuf_base + 63) // 64 * 64
    top = nc.sbuf_top
    cur = [base]

    def alloc(name, shape, dt, at=None):
        nbytes = int(np.prod(shape[1:])) * (4 if dt in (F32, I32) else 2)
        nbytes = (nbytes + 63) // 64 * 64
        if at is None:
            off = cur[0]
            cur[0] += nbytes
        else:
            off = at[0]
            at[0] += nbytes
        assert off + nbytes <= top, (name, off, nbytes, top)
        return nc.alloc_sbuf_tensor_at(name, list(shape), dt, offset=off)

    SL = [alloc(f"sl{i}", [128, 4096], BF16) for i in range(NS)]
    SLb = [P.buf(f"sl{i}", dma=True) for i in range(NS)]
    CST = alloc("cst_sb", [128, NCST], F32)
    CSTb = P.buf("cst", dma=True)
    POSI = alloc("posi_sb", [128, 16], I32)
    IDENT = alloc("ident", [128, 128], BF16)
    TRI = alloc("tri", [128, 128], BF16)
    ONESB = alloc("onesb", [128, 128], BF16)
    ONESF = alloc("onesf", [128, 128], F32)
    ONESC = alloc("onesc", [128, 128], BF16)
    COS4 = alloc("cos4", [128, 16, 4, 16], F32)
    SIN4 = alloc("sin4", [128, 16, 4, 16], F32)
    NLAM = alloc("nlam", [128, 1], F32)
    G8 = alloc("g8", [128, 256], F32)
    AG1 = alloc("ag1", [128, 32], F32)
    AB1 = alloc("ab1", [128, 32], F32)
    EPSC = alloc("epsc", [128, 1], F32)
    SETb = P.buf("setup")
    phase_base = cur[0]

    PBt = [nc.alloc_psum_tensor(f"pb{i}", [128, 512], F32) for i in range(6)]
    PBb = [P.buf(f"pb{i}") for i in range(6)]
    TBt = [nc.alloc_psum_tensor(f"tb{i}", [128, 1024], BF16) for i in range(2)]
    TBb = [P.buf(f"tb{i}") for i in range(2)]
    pbi = [0]
    tbi = [0]

    pinned = set()

    def next_pb():
        while True:
            i = pbi[0] % 6
            pbi[0] += 1
            if i not in pinned:
                return i

    def next_tb():
        i = tbi[0] % 2
        tbi[0] += 1
        return i

    nload = [0]

    def load_slot(widx):
        s = nload[0] % NS
        nload[0] += 1
        fns = []
        for hh in range(2):
            def fn(e, hh=hh, s=s, widx=widx):
                return e.dma_start(out=SL[s][:, hh * 2048:(hh + 1) * 2048],
                                   in_=wst[widx * 128:(widx + 1) * 128, hh * 2048:(hh + 1) * 2048])
            fns.append(fn)
        P.dma("pool", fns, reads=[], writes=[SLb[s]], owner=SLb[s])
        return s

    pa = [phase_base]
    XT = alloc("xt", [128, 32, NT], BF16, pa)
    XTq = [P.buf(f"xtq{k}", dma=True) for k in range(4)]
    pa1 = [pa[0]]
    STG = [alloc(f"stg{i}", [128, 4128], BF16, pa1) for i in range(2)]
    STGb = [P.buf(f"stg{i}", dma=True) for i in range(2)]
    KB = [alloc(f"kb{i}", [128, 512], BF16, pa1) for i in range(2)]
    KBb = [P.buf(f"kb{i}") for i in range(2)]
    RT = [[alloc(f"rt{i}_{k}", [128, 4, 16], F32, pa1) for k in range(4)] for i in range(2)]
    RTb = [[P.buf(f"rt{i}_{k}") for k in range(4)] for i in range(2)]

    P.dma("sp", [lambda e: e.dma_start(out=CST[:, :], in_=cst),
                 lambda e: e.dma_start(out=POSI[:, :], in_=posi)], reads=[], writes=[CSTb], owner=CSTb)
    sat = [pa1[0]]
    POSF = alloc("posf", [128, 16], F32, sat)
    ANG = alloc("ang", [128, 16, 16], F32, sat)
    V1 = alloc("v1", [128, 256], F32, sat)
    VI = alloc("vi", [128, 256], I32, sat)
    VF = alloc("vf", [128, 256], F32, sat)
    RR_ = alloc("rr", [128, 256], F32, sat)
    GT_ = alloc("gt", [128, 256], F32, sat)
    CS = alloc("cs", [128, 16, 16], F32, sat)
    PR = alloc("pr", [128, 128], F32, sat)
    DD = alloc("dd", [128, 4], F32, sat)
    Sb = P.buf("setup_tmp")
    R = [CSTb, Sb, SETb]
    Wr = [Sb, SETb]

    def sop(eng, fn):
        P.op(eng, fn, reads=R, writes=Wr)

    sop("dve", lambda e: e.tensor_copy(out=IDENT[:, :], in_=CST[:, C_IDENT:C_IDENT + 128]))
    sop("dve", lambda e: e.tensor_copy(out=TRI[:, :], in_=CST[:, C_TRI:C_TRI + 128]))
    sop("dve", lambda e: e.memset(ONESB[:, :], 1.0))
    sop("dve", lambda e: e.memset(ONESC[:, :], 1.0 / 2048.0))
    sop("dve", lambda e: e.memset(ONESF[:, :], 1.0 / D))
    sop("dve", lambda e: e.memset(EPSC[:, :], EPS))
    sop("dve", lambda e: e.tensor_copy(out=POSF[:, :], in_=POSI[:, :]))
    for j in range(16):
        sop("dve", lambda e, j=j: e.tensor_scalar(ANG[:, j, :], CST[:, C_INVF:C_INVF + 16], POSF[:, j:j + 1], None, ALU.mult))
    angf = ANG[:, :, :].rearrange("p a b -> p (a b)")
    for which, shift, DST in ((0, 0.0, SIN4), (1, 0.25, COS4)):
        sop("dve", lambda e, shift=shift: e.tensor_scalar(V1[:, :], angf, 1.0 / (2 * math.pi), shift, ALU.mult, ALU.add))
        sop("dve", lambda e: e.tensor_copy(out=VI[:, :], in_=V1[:, :]))
        sop("dve", lambda e: e.tensor_copy(out=VF[:, :], in_=VI[:, :]))
        sop("dve", lambda e: e.tensor_tensor(out=RR_[:, :], in0=V1[:, :], in1=VF[:, :], op=ALU.subtract))
        sop("dve", lambda e: e.tensor_scalar(GT_[:, :], RR_[:, :], 0.5, None, ALU.is_gt))
        sop("dve", lambda e: e.tensor_tensor(out=RR_[:, :], in0=RR_[:, :], in1=GT_[:, :], op=ALU.subtract))
        sop("dve", lambda e: e.tensor_scalar(GT_[:, :], RR_[:, :], -0.5, None, ALU.is_lt))
        sop("dve", lambda e: e.tensor_tensor(out=RR_[:, :], in0=RR_[:, :], in1=GT_[:, :], op=ALU.add))
        sop("act", lambda e: e.activation(out=CS[:, :, :].rearrange("p a b -> p (a b)"), in_=RR_[:, :], func=AF.Sin,
                                          scale=2 * math.pi * (1 - 2e-6)))
        for c in range(4):
            sop("dve", lambda e, c=c, DST=DST: e.tensor_copy(out=DST[:, :, c, :], in_=CS[:, :, :]))
    for k in range(2):
        sop("dve", lambda e, k=k: e.tensor_tensor(out=PR[:, :], in0=CST[:, C_LAMV + 256 * k:C_LAMV + 256 * k + 128],
                                                  in1=CST[:, C_LAMV + 256 * k + 128:C_LAMV + 256 * k + 256], op=ALU.mult))
        sop("dve", lambda e, k=k: e.reduce_sum(out=DD[:, k:k + 1], in_=PR[:, :], axis=AX.X))
    sop("act", lambda e: e.activation(out=DD[:, 2:4], in_=DD[:, 0:2], func=AF.Exp))
    sop("dve", lambda e: e.tensor_tensor(out=NLAM[:, :], in0=DD[:, 3:4], in1=DD[:, 2:3], op=ALU.subtract))
    sop("dve", lambda e: e.tensor_scalar(NLAM[:, :], NLAM[:, :], -LAMBDA_INIT, None, ALU.add))
    sop("dve", lambda e: e.tensor_scalar(G8[:, :], CST[:, C_SUBG:C_SUBG + 256], 1.0 - LAMBDA_INIT, None, ALU.mult))
    sop("dve", lambda e: e.tensor_scalar(AG1[:, :], CST[:, C_LN1G:C_LN1G + 32], ALPHA, None, ALU.mult))
    sop("dve", lambda e: e.tensor_scalar(AB1[:, :], CST[:, C_LN1B:C_LN1B + 32], ALPHA, None, ALU.mult))
    KR = [CSTb, SETb]

    def load_xt(src, widx0=None):
        v = src.rearrange("(c p) t -> p c t", p=128)
        slots = []
        for k in range(4):
            if widx0 is not None:
                slots.append(load_slot(widx0 + k))
            fns = []
            for c2 in range(4 * k, 4 * k + 4):
                fns.append(lambda e, c2=c2: e.dma_start(out=XT[:, 2 * c2:2 * c2 + 2, :], in_=v[:, 2 * c2:2 * c2 + 2, :]))
            P.dma("pool", fns, reads=[], writes=[XTq[k]], owner=XTq[k], phase_mem=True)
        return slots

    stgi = [0]
    kbi = [0]

    def proj_x(slots, tt):
        bk = next_pb()
        for k in range(4):
            def fn(e, k=k):
                for kc in range(8 * k, 8 * k + 8):
                    ins = e.matmul(PBt[bk][:, :], lhsT=XT[:, kc, tt * 128:(tt + 1) * 128],
                                   rhs=SL[slots[k]][:, (kc % 8) * 512:(kc % 8 + 1) * 512],
                                   start=(kc == 0), stop=(kc == 31))
                return ins
            P.op("pe", fn, reads=[XTq[k], SLb[slots[k]]], writes=[PBb[bk]])
        return bk

    def proj_x_qmajor(slots, tts):
        bks = [next_pb() for _ in tts]
        for k in range(4):
            for tt, bk in zip(tts, bks):
                def fn(e, k=k, tt=tt, bk=bk):
                    for kc in range(8 * k, 8 * k + 8):
                        ins = e.matmul(PBt[bk][:, :], lhsT=XT[:, kc, tt * 128:(tt + 1) * 128],
                                       rhs=SL[slots[k]][:, (kc % 8) * 512:(kc % 8 + 1) * 512],
                                       start=(kc == 0), stop=(kc == 31))
                    return ins
                P.op("pe", fn, reads=[XTq[k], SLb[slots[k]]], writes=[PBb[bk]])
        return bks

    def rope_evac(bk, j):
        x = kbi[0] % 2
        kbi[0] += 1
        v = PBt[bk][:, :].rearrange("p (c d) -> p c d", c=4)
        kb = KB[x][:, :].rearrange("p (c d) -> p c d", c=4)
        t1, t2 = v[:, :, 0:16], v[:, :, 16:32]
        cs, sn = COS4[:, j, :, :], SIN4[:, j, :, :]
        rt, rtb = RT[x], RTb[x]
        P.op("dve", lambda e: e.tensor_tensor(out=rt[0][:, :, :], in0=t1, in1=cs, op=ALU.mult), reads=[PBb[bk]] + KR, writes=[rtb[0]])
        P.op("dve", lambda e: e.tensor_tensor(out=rt[1][:, :, :], in0=t2, in1=sn, op=ALU.mult), reads=[PBb[bk]] + KR, writes=[rtb[1]])
        P.op("dve", lambda e: e.tensor_tensor(out=rt[2][:, :, :], in0=t2, in1=cs, op=ALU.mult), reads=[PBb[bk]] + KR, writes=[rtb[2]])
        P.op("dve", lambda e: e.tensor_tensor(out=rt[3][:, :, :], in0=t1, in1=sn, op=ALU.mult), reads=[PBb[bk]] + KR, writes=[rtb[3]])
        P.op("dve", lambda e: e.tensor_tensor(out=kb[:, :, 0:16], in0=rt[0][:, :, :], in1=rt[1][:, :, :], op=ALU.subtract),
             reads=[rtb[0], rtb[1]], writes=[KBb[x]])
        P.op("dve", lambda e: e.tensor_tensor(out=kb[:, :, 16:32], in0=rt[2][:, :, :], in1=rt[3][:, :, :], op=ALU.add),
             reads=[rtb[2], rtb[3]], writes=[KBb[x]])
        P.op("act", lambda e: e.activation(out=kb[:, :, 32:128], in_=v[:, :, 32:128], func=AF.Copy),
             reads=[PBb[bk]], writes=[KBb[x]])
        return x

    def transposes_to_stage(x, y, tt):
        tb = next_tb()

        def fn(e):
            for c in range(4):
                ins = e.transpose(out=TBt[tb][:, c * 128:(c + 1) * 128], in_=KB[x][:, c * 128:(c + 1) * 128], identity=IDENT[:, :])
            return ins
        P.op("pe", fn, reads=[KBb[x]] + KR, writes=[TBb[tb]])
        dst = STG[y][:, 0:4096].rearrange("p (c t) -> p c t", c=4)[:, :, tt * 128:(tt + 1) * 128]
        src = TBt[tb][:, 0:512].rearrange("p (c t) -> p c t", c=4)
        P.op("act", lambda e: e.activation(out=dst, in_=src, func=AF.Copy), reads=[TBb[tb]], writes=[STGb[y]])

    def qk_panel(widx0, dst, comp0, sidx, tcol0, slots=None, qmajor=False):
        if slots is None:
            slots = [load_slot(widx0 + kq) for kq in range(4)]
        y = stgi[0] % 2
        stgi[0] += 1
        prev = None
        pre_bks = proj_x_qmajor(slots, list(range(6))) if qmajor else []
        for tt in range(8):
            bk = pre_bks[tt] if tt < len(pre_bks) else proj_x(slots, tt)
            if prev is not None:
                transposes_to_stage(prev[0], y, prev[1])
            x = rope_evac(bk, sidx * 8 + tt)
            prev = (x, tt)
            drain(max(0, -(-(len(conv_q) - 24) // (8 - tt))))
        transposes_to_stage(prev[0], y, prev[1])
        dv = dst[comp0:comp0 + 4, :, tcol0:tcol0 + NT].rearrange("c d t -> d c t")
        P.dma("sp", [lambda e: e.dma_start(out=dv, in_=STG[y][:, 0:4096].rearrange("p (c t) -> p c t", c=4))],
              reads=[STGb[y]], writes=[], owner=STGb[y])

    def v_panel(widx0, h0, sidx):
        slots = [load_slot(widx0 + kq) for kq in range(4)]
        y = stgi[0] % 2
        stgi[0] += 1
        sv = STG[y][:, :].rearrange("p (t h c) -> p t h c", t=8, h=2)
        for tt in range(8):
            bk = proj_x(slots, tt)
            P.op("act", lambda e, bk=bk, tt=tt: e.activation(out=sv[:, tt, :, 0:256],
                                                             in_=PBt[bk][:, :].rearrange("p (h c) -> p h c", h=2), func=AF.Copy),
                 reads=[PBb[bk]], writes=[STGb[y]])
            for hh in range(2):
                P.op("dve", lambda e, tt=tt, hh=hh: e.tensor_copy(out=sv[:, tt, hh, 256:257],
                                                                  in_=CST[:, C_FLAGS + sidx:C_FLAGS + sidx + 1]),
                     reads=KR, writes=[STGb[y]])
            drain(max(0, -(-(len(conv_q) - 24) // (8 - tt))))
        P.dma("sp", [lambda e, hh=hh: e.dma_start(out=vsd[h0 + hh, sidx * 8:(sidx + 1) * 8, :, 0:257].rearrange("t p c -> p t c"),
                                                   in_=sv[:, :, hh, 0:257]) for hh in range(2)],
              reads=[STGb[y]], writes=[], owner=STGb[y])

    pa2 = [sat[0]]
    pa2_start = sat[0]
    XTH = alloc("xth_sb", [128, 32, 32], BF16, pa2)
    XTHb = P.buf("xth", dma=True)
    CC = [alloc(f"cc{i}", [128, 1056], F32, pa2) for i in range(2)]
    CCb = [P.buf(f"cc{i}") for i in range(2)]
    SG = [alloc(f"sg{i}", [128, 1056], F32, pa2) for i in range(2)]
    SGb = [P.buf(f"sg{i}") for i in range(2)]
    ACC2 = [alloc(f"acc2_{i}", [128, 1024], F32, pa2) for i in range(2)]
    ACC2b = [P.buf(f"acc2_{i}") for i in range(2)]
    CVS = [alloc(f"cvs{i}", [128, 1024], BF16, pa2) for i in range(2)]
    CVSb = [P.buf(f"cvs{i}", dma=True) for i in range(2)]
    NP = 0
    glu_slots = {}
    conv_q = []
    conv_done = [0]

    def drain(n):
        n = min(n, len(conv_q))
        for _ in range(n):
            conv_q.pop(0)()
        conv_done[0] += n

    def glu_group(g):
        while conv_done[0] < 32 * (g - 1):
            drain(1)
        if g not in glu_slots:
            glu_slots[g] = (load_slot(W_A + g), load_slot(W_G + g))
        sa, sg_ = glu_slots[g]
        x = g % 2
        banks = {}
        for hh in range(2):
            for nm, s in (("a", sa), ("g", sg_)):
                bk = next_pb()
                banks[(nm, hh)] = bk

                def fn(e, bk=bk, s=s, hh=hh):
                    for kc in range(32):
                        ins = e.matmul(PBt[bk][:, :], lhsT=SL[s][:, kc * 128:(kc + 1) * 128],
                                       rhs=XT[:, kc, hh * 512:(hh + 1) * 512], start=(kc == 0), stop=(kc == 31))
                    return ins
                P.op("pe", fn, reads=XTq + [SLb[s]], writes=[PBb[bk]])
            ga, gg = banks[("a", hh)], banks[("g", hh)]
            P.op("act", lambda e, gg=gg, hh=hh, x=x, g=g: e.activation(
                out=SG[x][:, 32 + hh * 512:32 + (hh + 1) * 512], in_=PBt[gg][:, :], func=AF.Sigmoid,
                bias=CST[:, C_BGLU + 16 + g:C_BGLU + 17 + g], scale=1.0),
                reads=[PBb[gg]] + KR, writes=[SGb[x]])
            P.op("dve", lambda e, ga=ga, hh=hh, x=x, g=g: e.scalar_tensor_tensor(
                out=CC[x][:, 32 + hh * 512:32 + (hh + 1) * 512], in0=PBt[ga][:, :],
                scalar=CST[:, C_BGLU + g:C_BGLU + g + 1], in1=SG[x][:, 32 + hh * 512:32 + (hh + 1) * 512],
                op0=ALU.add, op1=ALU.mult), reads=[PBb[ga], SGb[x]] + KR, writes=[CCb[x]])
        hb = next_pb()

        def fnh(e, hb=hb, sa=sa, sg_=sg_):
            for nmi, s in enumerate((sa, sg_)):
                for kc in range(32):
                    ins = e.matmul(PBt[hb][:, nmi * 32:(nmi + 1) * 32], lhsT=SL[s][:, kc * 128:(kc + 1) * 128],
                                   rhs=XTH[:, kc, :], start=(kc == 0), stop=(kc == 31))
            return ins
        P.op("pe", fnh, reads=[XTHb, SLb[sa], SLb[sg_]], writes=[PBb[hb]])
        P.op("act", lambda e, hb=hb, x=x, g=g: e.activation(out=SG[x][:, 0:32], in_=PBt[hb][:, 32:64], func=AF.Sigmoid,
                                                            bias=CST[:, C_BGLU + 16 + g:C_BGLU + 17 + g], scale=1.0),
             reads=[PBb[hb]] + KR, writes=[SGb[x]])
        P.op("dve", lambda e, hb=hb, x=x, g=g: e.scalar_tensor_tensor(
            out=CC[x][:, 0:32], in0=PBt[hb][:, 0:32], scalar=CST[:, C_BGLU + g:C_BGLU + g + 1], in1=SG[x][:, 0:32],
            op0=ALU.add, op1=ALU.mult), reads=[PBb[hb], SGb[x]] + KR, writes=[CCb[x]])
        P.op("dve", lambda e, x=x: e.tensor_scalar(CC[x][:, 0:32], CC[x][:, 0:32], CST[:, C_FLAGS:C_FLAGS + 1], None, ALU.mult),
             reads=[CCb[x]] + KR, writes=[CCb[x]])
        cw = lambda j, g=g: CST[:, C_CONVW + g * 31 + j:C_CONVW + g * 31 + j + 1]
        conv_q.append(lambda: P.op("dve", lambda e: e.tensor_scalar(ACC2[x][:, :], CC[x][:, 2:1026], cw(0), CST[:, C_CONVB + g:C_CONVB + g + 1],
                                                            ALU.mult, ALU.add), reads=[CCb[x]] + KR, writes=[ACC2b[x]]))
        for j in range(1, 30):
            conv_q.append(lambda j=j: P.op("dve", lambda e: e.scalar_tensor_tensor(out=ACC2[x][:, :], in0=CC[x][:, 2 + j:1026 + j], scalar=cw(j),
                                                                               in1=ACC2[x][:, :], op0=ALU.mult, op1=ALU.add),
                                           reads=[CCb[x], ACC2b[x]] + KR, writes=[ACC2b[x]]))
        conv_q.append(lambda: P.op("dve", lambda e: e.scalar_tensor_tensor(out=CVS[x][:, :], in0=CC[x][:, 32:1056], scalar=cw(30),
                                                                       in1=ACC2[x][:, :], op0=ALU.mult, op1=ALU.add),
                                   reads=[CCb[x], ACC2b[x]] + KR, writes=[CVSb[x]]))
        conv_q.append(lambda: P.dma("sp", [lambda e: e.dma_start(out=cvd[g, :, :], in_=CVS[x][:, :])], reads=[CVSb[x]], writes=[], owner=CVSb[x]))

    glu_next = [0]

    def glu_after_panel(pi):
        n = [2, 2, 2, 2, 2, 1, 1, 1, 1, 1, 1, 0][pi]
        for _ in range(n):
            glu_group(glu_next[0])
            glu_next[0] += 1

    for sidx, src in ((0, xtp), (1, xto)):
        pre = [load_slot(W_Q + kq) for kq in range(4)] if sidx == 1 else None
        pre_k = load_xt(src, W_K if sidx == 0 else None)
        pi = 0
        if sidx == 1:
            P.dma("pool", [lambda e: e.dma_start(out=XTH[:, :, :], in_=xth.rearrange("(c p) t -> p c t", p=128))],
                  reads=[], writes=[XTHb], owner=XTHb, phase_mem=True)
            for hp in range(4):
                qk_panel(W_Q + 4 * hp, qts, 4 * hp, sidx, 0, slots=pre if hp == 0 else None, qmajor=(hp == 0))
                glu_after_panel(pi)
                pi += 1
        for hp in range(4):
            qk_panel(W_K + 4 * hp, kts, 4 * hp, sidx, sidx * NT, slots=pre_k if (hp == 0 and sidx == 0) else None, qmajor=(hp == 0 and sidx == 0))
            if sidx == 1:
                glu_after_panel(pi)
                pi += 1
        for hp in range(4):
            v_panel(W_V + 4 * hp, 2 * hp, sidx)
            if sidx == 1:
                glu_after_panel(pi)
                pi += 1
    assert glu_next[0] == 16
    drain(len(conv_q))
    if stop_after == "A2":
        return finish(nc, P)
    own_end = P.snapshot()
    P.barrier_partial(("pe",), CVSb)
    cvp = [pa[0] + (pa2_start - pa[0])]
    CV = alloc("cv", [128, 16, NT], BF16, cvp)
    CVb = P.buf("cv", dma=True)
    P.dma("sp", [lambda e, q=q: e.dma_start(out=CV[:, 4 * q:4 * q + 4, :], in_=cvd[4 * q:4 * q + 4, :, :].rearrange("g p t -> p g t"))
                 for q in range(4)], reads=[], writes=[CVb], owner=CVb, extra=own_end)

    pb_ = [phase_base]
    pb2 = [pa[0]]
    QTH = [alloc(f"qth{i}", [128, 2, NT], BF16, pb_) for i in range(2)]
    KTH = [alloc(f"kth{i}", [128, 2, 2 * NT], BF16, pb_) for i in range(2)]
    VH = [alloc(f"vh{i}", [128, 16, 258], BF16, pb_) for i in range(2)]
    QTHb = [P.buf(f"qth{i}", dma=True) for i in range(2)]
    KTHb = [P.buf(f"kth{i}", dma=True) for i in range(2)]
    VHb = [P.buf(f"vh{i}", dma=True) for i in range(2)]
    ET = [alloc(f"et{i}", [128, 2, 2048], BF16, pb_) for i in range(2)]
    ETb = [P.buf(f"et{i}") for i in range(2)]
    assert pb_[0] <= phase_base + 65536
    MST = [alloc(f"mst{i}", [128, 2, NT], BF16, pb2) for i in range(2)]
    MSTb = [P.buf(f"mst{i}", dma=True) for i in range(2)]
    sm = {}
    smb = {}
    for nm, shp, dt in (("rinv", [128, 2], F32), ("r2", [128, 1], F32), ("t1", [128, 256], F32), ("o", [128, 256], F32),
                        ("junk", [128, 256], F32), ("ss", [128, 1], F32), ("rs", [128, 1], F32), ("rr", [128, 1], F32),
                        ("on", [128, 256], BF16)):
        sm[nm] = [alloc(f"{nm}{i}", shp, dt, pb2) for i in range(2)]
        smb[nm] = [P.buf(f"{nm}{i}") for i in range(2)]

    def attn_head(h):
        x = h % 2
        P.dma("pool", [lambda e, h=h, x=x: e.dma_start(out=QTH[x][:, :, :], in_=qts[2 * h:2 * h + 2, :, :].rearrange("c d t -> d c t"))],
              reads=[], writes=[QTHb[x]], owner=QTHb[x], phase_mem=True)
        P.dma("pool", [lambda e, h=h, x=x: e.dma_start(out=KTH[x][:, :, :], in_=kts[2 * h:2 * h + 2, :, :].rearrange("c d t -> d c t"))],
              reads=[], writes=[KTHb[x]], owner=KTHb[x], phase_mem=True)
        P.dma("pool", [lambda e, h=h, x=x: e.dma_start(out=VH[x][:, :, 0:257], in_=vsd[h, :, :, 0:257].rearrange("t p c -> p t c"))],
              reads=[], writes=[VHb[x]], owner=VHb[x], phase_mem=True)
        z = h % 2
        obs = {}

        def QK(i):
            y = i % 2
            nj = 9 + i
            for m in range(2):
                for jb in range((nj + 3) // 4):
                    bk = next_pb()
                    js = list(range(jb * 4, min(nj, jb * 4 + 4)))

                    def fn(e, bk=bk, js=js, m=m, i=i):
                        for j in js:
                            ins = e.matmul(PBt[bk][:, (j % 4) * 128:(j % 4 + 1) * 128], lhsT=KTH[x][:, m, j * 128:(j + 1) * 128],
                                           rhs=QTH[x][:, m, i * 128:(i + 1) * 128], start=True, stop=True)
                        return ins
                    P.op("pe", fn, reads=[KTHb[x], QTHb[x]], writes=[PBb[bk]])
                    P.op("act", lambda e, bk=bk, js=js, m=m, y=y: e.activation(
                        out=ET[y][:, m, js[0] * 128:(js[-1] + 1) * 128], in_=PBt[bk][:, 0:len(js) * 128], func=AF.Exp, scale=SM_SCALE),
                        reads=[PBb[bk]], writes=[ETb[y]])
                jd = 8 + i
                P.op("dve", lambda e, m=m, y=y, jd=jd: e.tensor_tensor(out=ET[y][:, m, jd * 128:(jd + 1) * 128],
                                                                       in0=ET[y][:, m, jd * 128:(jd + 1) * 128], in1=TRI[:, :], op=ALU.mult),
                     reads=[ETb[y]] + KR, writes=[ETb[y]])

        def PV(i):
            y = i % 2
            nj = 9 + i
            ob = []
            for m in range(2):
                bk = next_pb()
                ob.append(bk)

                def fn(e, bk=bk, m=m, nj=nj, y=y):
                    for j in range(nj):
                        ins = e.matmul(PBt[bk][:, 0:257], lhsT=ET[y][:, m, j * 128:(j + 1) * 128], rhs=VH[x][:, j, 0:257],
                                       start=(j == 0), stop=(j == nj - 1))
                    return ins
                P.op("pe", fn, reads=[ETb[y], VHb[x]], writes=[PBb[bk]])
            T = {k: v[y] for k, v in sm.items()}
            Tb = {k: v[y] for k, v in smb.items()}
            for m in range(2):
                P.op("dve", lambda e, m=m, bk=ob[m]: e.reciprocal(out=T["rinv"][:, m:m + 1], in_=PBt[bk][:, 256:257]),
                     reads=[PBb[ob[m]]], writes=[Tb["rinv"]])
            P.op("dve", lambda e: e.tensor_tensor(out=T["r2"][:, :], in0=T["rinv"][:, 1:2], in1=NLAM[:, :], op=ALU.mult),
                 reads=[Tb["rinv"]] + KR, writes=[Tb["r2"]])
            P.op("dve", lambda e: e.tensor_scalar(T["t1"][:, :], PBt[ob[0]][:, 0:256], T["rinv"][:, 0:1], None, ALU.mult),
                 reads=[PBb[ob[0]], Tb["rinv"]], writes=[Tb["t1"]])
            P.op("dve", lambda e: e.scalar_tensor_tensor(out=T["o"][:, :], in0=PBt[ob[1]][:, 0:256], scalar=T["r2"][:, 0:1],
                                                         in1=T["t1"][:, :], op0=ALU.mult, op1=ALU.add),
                 reads=[PBb[ob[1]], Tb["r2"], Tb["t1"]], writes=[Tb["o"]])
            P.op("dve", lambda e: e.scalar_tensor_tensor(out=T["junk"][:, :], in0=T["o"][:, :], scalar=1.0, in1=T["o"][:, :],
                                                         op0=ALU.mult, op1=ALU.mult, accum_out=T["ss"][:, 0:1]),
                 reads=[Tb["o"]], writes=[Tb["junk"], Tb["ss"]])

        def POST2(i):
            y = i % 2
            T = {k: v[y] for k, v in sm.items()}
            Tb = {k: v[y] for k, v in smb.items()}
            P.op("act", lambda e: e.activation(out=T["rs"][:, :], in_=T["ss"][:, :], func=AF.Ln, scale=1.0 / 256.0, bias=EPSC[:, 0:1]),
                 reads=[Tb["ss"]] + KR, writes=[Tb["rs"]])
            P.op("act", lambda e: e.activation(out=T["rr"][:, :], in_=T["rs"][:, :], func=AF.Exp, scale=-0.5),
                 reads=[Tb["rs"]], writes=[Tb["rr"]])
            P.op("dve", lambda e: e.scalar_tensor_tensor(out=T["on"][:, :], in0=T["o"][:, :], scalar=T["rr"][:, 0:1], in1=G8[:, :],
                                                         op0=ALU.mult, op1=ALU.mult), reads=[Tb["o"], Tb["rr"]] + KR, writes=[Tb["on"]])

        def TR(i):
            y = i % 2
            tb = next_tb()

            def fn(e):
                for c in range(2):
                    ins = e.transpose(out=TBt[tb][:, c * 128:(c + 1) * 128], in_=sm["on"][y][:, c * 128:(c + 1) * 128], identity=IDENT[:, :])
                return ins
            P.op("pe", fn, reads=[smb["on"][y]] + KR, writes=[TBb[tb]])
            P.op("act", lambda e: e.activation(out=MST[z][:, :, i * 128:(i + 1) * 128],
                                               in_=TBt[tb][:, 0:256].rearrange("p (c t) -> p c t", c=2), func=AF.Copy),
                 reads=[TBb[tb]], writes=[MSTb[z]])

        for i in range(11):
            if i < 8:
                QK(i)
            if 1 <= i <= 8:
                PV(i - 1)
            if 2 <= i <= 9:
                POST2(i - 2)
            if i >= 3:
                TR(i - 3)
        P.dma("sp", [lambda e, h=h, z=z: e.dma_start(out=mts[2 * h:2 * h + 2, :, :].rearrange("c d t -> d c t"), in_=MST[z][:, :, :])],
              reads=[MSTb[z]], writes=[], owner=MSTb[z])

    assert pb2[0] <= pa1[0], (pb2[0], pa1[0])
    for h in range(8):
        attn_head(h)
    P.barrier()
    if stop_after == "B":
        return finish(nc, P)

    pd = [phase_base]
    MH = alloc("mh", [128, 32, 512], BF16, pd)
    MHb = P.buf("mh", dma=True)
    MHc = [P.buf(f"mhc{k}") for k in range(32)]
    P.dma("sp", [lambda e, q=q: e.dma_start(out=MH[:, 8 * q:8 * q + 8, :], in_=mts[8 * q:8 * q + 8, :, 0:512].rearrange("c p t -> p c t"))
                 for q in range(2)], reads=[], writes=MHc[0:16], owner=MHb)
    pc_ = [pd[0]]
    assert pc_[0] + 49152 <= pa2_start
    SQ = alloc("sq", [128, 16, NT], BF16, pc_)
    SQbs = [P.buf(f"sq{g}") for g in range(16)]
    TMP = alloc("tmpb", [128, NT], F32, pc_)
    STb = P.buf("stats")
    YT = [alloc(f"yt{i}", [128, NT], F32, pc_) for i in range(2)]
    YTb = [P.buf(f"yt{i}") for i in range(2)]
    MS = [alloc(f"ms{i}", [128, NT], BF16, pc_) for i in range(2)]
    MSb = [P.buf(f"ms{i}", dma=True) for i in range(2)]
    for g in range(16):
        if g % 2 == 0:
            P.op("act", lambda e, g=g: e.activation(out=SQ[:, g, :], in_=CV[:, g, :], func=AF.Square), reads=[CVb], writes=[SQbs[g]])
        else:
            P.op("dve", lambda e, g=g: e.tensor_tensor(out=SQ[:, g, :], in0=CV[:, g, :], in1=CV[:, g, :], op=ALU.mult), reads=[CVb], writes=[SQbs[g]])
    sbk = {}
    for nm, SRC, srcb in (("s", CV, [CVb]), ("q", SQ, SQbs)):
        for hh in range(2):
            bk = next_pb()
            sbk[(nm, hh)] = bk

            def fn(e, bk=bk, SRC=SRC, hh=hh):
                for g in range(16):
                    ins = e.matmul(PBt[bk][:, :], lhsT=ONESC[:, :], rhs=SRC[:, g, hh * 512:(hh + 1) * 512], start=(g == 0), stop=(g == 15))
                return ins
            P.op("pe", fn, reads=srcb + KR, writes=[PBb[bk]])
    for hh in range(2):
        sl = slice(hh * 512, (hh + 1) * 512)
        bs, bq = sbk[("s", hh)], sbk[("q", hh)]
        P.op("act", lambda e, sl=sl, bs=bs: e.activation(out=TMP[:, sl], in_=PBt[bs][:, :], func=AF.Square), reads=[PBb[bs]], writes=[STb])
        P.op("dve", lambda e, sl=sl, bq=bq: e.tensor_tensor(out=TMP[:, sl], in0=PBt[bq][:, :], in1=TMP[:, sl], op=ALU.subtract), reads=[PBb[bq], STb], writes=[STb])
        P.op("act", lambda e, sl=sl: e.activation(out=TMP[:, sl], in_=TMP[:, sl], func=AF.Sqrt, bias=EPSC[:, 0:1], scale=1.0), reads=[STb] + KR, writes=[STb])
        P.op("dve", lambda e, sl=sl, bq=bq: e.reciprocal(out=PBt[bq][:, :], in_=TMP[:, sl]), reads=[STb], writes=[PBb[bq]])
    for g in range(16):
        x = g % 2
        for hh in range(2):
            sl = slice(hh * 512, (hh + 1) * 512)
            bs, bq = sbk[("s", hh)], sbk[("q", hh)]
            P.op("dve", lambda e, g=g, x=x, sl=sl, bs=bs: e.tensor_tensor(out=YT[x][:, sl], in0=CV[:, g, sl], in1=PBt[bs][:, :], op=ALU.subtract),
                 reads=[CVb, PBb[bs]], writes=[YTb[x]])
            P.op("dve", lambda e, x=x, sl=sl, bq=bq: e.tensor_tensor(out=YT[x][:, sl], in0=YT[x][:, sl], in1=PBt[bq][:, :], op=ALU.mult),
                 reads=[YTb[x], PBb[bq]], writes=[YTb[x]])
        P.op("act", lambda e, g=g, x=x: e.activation(out=MH[:, 16 + g, :], in_=YT[x][:, 0:512], func=AF.Silu,
                                                     scale=CST[:, C_CLNG + g:C_CLNG + g + 1], bias=CST[:, C_CLNB + g:C_CLNB + g + 1]),
             reads=[YTb[x]] + KR, writes=[MHc[16 + g]])
        P.op("act", lambda e, g=g, x=x: e.activation(out=MS[x][:, 0:512], in_=YT[x][:, 512:1024], func=AF.Silu,
                                                     scale=CST[:, C_CLNG + g:C_CLNG + g + 1], bias=CST[:, C_CLNB + g:C_CLNB + g + 1]),
             reads=[YTb[x]] + KR, writes=[MSb[x]])
        P.dma("sp", [lambda e, g=g, x=x: e.dma_start(out=mts[16 + g, :, 512:1024], in_=MS[x][:, 0:512])], reads=[MSb[x]], writes=[], owner=MSb[x])
    P.barrier_targets(P.snapshot(), ("act", "dve", "sp"))
    if stop_after == "B2":
        return finish(nc, P)

    ACC = alloc("acc", [128, 32, 512], F32, pd)
    ACCb = [P.buf(f"acc{i}") for i in range(32)]
    OUTb = P.buf("outdma", dma=True)
    HID = [alloc(f"hid{i}", [128, FBG, 512], BF16, pd) for i in range(2)]
    HIDb = [P.buf(f"hid{i}") for i in range(2)]
    XRES = [alloc(f"xres{i}", [128, 512], F32, pd) for i in range(2)]
    XRESb = [P.buf(f"xres{i}", dma=True) for i in range(2)]
    LTMP = alloc("ltmp", [128, 512], F32, pd)
    SUMV = alloc("sumv", [128, 512], F32, pd)
    SQV = alloc("sqv", [128, 512], F32, pd)
    SUMVb = P.buf("sumv")
    SQVb = P.buf("sqv")
    OST = [alloc(f"ost{i}", [128, 512], F32, pd) for i in range(2)]
    OSTb = [P.buf(f"ost{i}", dma=True) for i in range(2)]
    LSTb = P.buf("lstats")
    SQC = [alloc(f"sqc{i}", [128, 512], F32, pd) for i in range(2)]
    SQCb = [P.buf(f"sqc{i}") for i in range(2)]
    SGC = [alloc(f"sgc{i}", [128, 512], F32, pd) for i in range(2)]
    SGCb = [P.buf(f"sgc{i}") for i in range(2)]
    xtov = xto.rearrange("(c p) t -> p c t", p=128)
    ytv = yt.rearrange("(c p) t -> p c t", p=128)

    def ln_finish(bs, bq):
        P.op("act", lambda e: e.activation(out=LTMP[:, :], in_=PBt[bs][:, :], func=AF.Square), reads=[PBb[bs]], writes=[LSTb])
        P.op("dve", lambda e: e.tensor_tensor(out=LTMP[:, :], in0=PBt[bq][:, :], in1=LTMP[:, :], op=ALU.subtract), reads=[PBb[bq], LSTb], writes=[LSTb])
        P.op("act", lambda e: e.activation(out=LTMP[:, :], in_=LTMP[:, :], func=AF.Sqrt, bias=EPSC[:, 0:1], scale=1.0), reads=[LSTb] + KR, writes=[LSTb])
        P.op("dve", lambda e: e.reciprocal(out=PBt[bq][:, :], in_=LTMP[:, :]), reads=[LSTb], writes=[PBb[bq]])

    def ln_stats_mm():
        bs, bq = next_pb(), next_pb()
        for dc in range(32):
            x = dc % 2
            P.op("act", lambda e, dc=dc, x=x: e.activation(out=SQC[x][:, :], in_=ACC[:, dc, :], func=AF.Square), reads=[ACCb[dc]], writes=[SQCb[x]])
            P.op("pe", lambda e, dc=dc: e.matmul(PBt[bs][:, :], lhsT=ONESF[:, :], rhs=ACC[:, dc, :], start=(dc == 0), stop=(dc == 31)),
                 reads=[ACCb[dc]] + KR, writes=[PBb[bs]])
            P.op("pe", lambda e, dc=dc, x=x: e.matmul(PBt[bq][:, :], lhsT=ONESF[:, :], rhs=SQC[x][:, :], start=(dc == 0), stop=(dc == 31)),
                 reads=[SQCb[x]] + KR, writes=[PBb[bq]])
        ln_finish(bs, bq)
        return bs, bq

    def ln_accum(dc):
        x = dc % 2
        if dc == 0:
            P.op("dve", lambda e: e.tensor_copy(out=SUMV[:, :], in_=ACC[:, 0, :]), reads=[ACCb[0]], writes=[SUMVb])
            P.op("act", lambda e: e.activation(out=SQV[:, :], in_=ACC[:, 0, :], func=AF.Square), reads=[ACCb[0]], writes=[SQVb])
        else:
            P.op("dve", lambda e: e.tensor_tensor(out=SUMV[:, :], in0=SUMV[:, :], in1=ACC[:, dc, :], op=ALU.add), reads=[ACCb[dc], SUMVb], writes=[SUMVb])
            P.op("act", lambda e: e.activation(out=SQC[x][:, :], in_=ACC[:, dc, :], func=AF.Square), reads=[ACCb[dc]], writes=[SQCb[x]])
            P.op("dve", lambda e: e.tensor_tensor(out=SQV[:, :], in0=SQV[:, :], in1=SQC[x][:, :], op=ALU.add), reads=[SQCb[x], SQVb], writes=[SQVb])

    def ln_stats_acc():
        bs, bq = next_pb(), next_pb()
        P.op("pe", lambda e: e.matmul(PBt[bs][:, :], lhsT=ONESF[:, :], rhs=SUMV[:, :], start=True, stop=True), reads=[SUMVb] + KR, writes=[PBb[bs]])
        P.op("pe", lambda e: e.matmul(PBt[bq][:, :], lhsT=ONESF[:, :], rhs=SQV[:, :], start=True, stop=True), reads=[SQVb] + KR, writes=[PBb[bq]])
        ln_finish(bs, bq)
        return bs, bq

    def ln_norm(dc, bs, bq):
        P.op("dve", lambda e: e.tensor_tensor(out=ACC[:, dc, :], in0=ACC[:, dc, :], in1=PBt[bs][:, :], op=ALU.subtract), reads=[ACCb[dc], PBb[bs]], writes=[ACCb[dc]])
        P.op("dve", lambda e: e.tensor_tensor(out=ACC[:, dc, :], in0=ACC[:, dc, :], in1=PBt[bq][:, :], op=ALU.mult), reads=[ACCb[dc], PBb[bq]], writes=[ACCb[dc]])

    def mh_load(tsl):
        P.dma("sp", [lambda e, q=q, tsl=tsl: e.dma_start(out=MH[:, 8 * q:8 * q + 8, :], in_=mts[8 * q:8 * q + 8, :, tsl].rearrange("c p t -> p c t"))
                     for q in range(4)], reads=[], writes=MHc, owner=MHb)

    def c1_chunk(dc, tsl):
        s = load_slot(W_O + dc)
        x = dc % 2
        P.dma("sp", [lambda e: e.dma_start(out=XRES[x][:, :], in_=xtov[:, dc, tsl])], reads=[], writes=[XRESb[x]], owner=XRESb[x])
        bk = next_pb()

        def fn(e):
            for kc in range(32):
                ins = e.matmul(PBt[bk][:, :], lhsT=SL[s][:, kc * 128:(kc + 1) * 128], rhs=MH[:, kc, :], start=(kc == 0), stop=(kc == 31))
            return ins
        P.op("pe", fn, reads=MHc + [SLb[s]], writes=[PBb[bk]])
        P.op("dve", lambda e: e.scalar_tensor_tensor(out=ACC[:, dc, :], in0=XRES[x][:, :], scalar=ALPHA, in1=PBt[bk][:, :],
                                                     op0=ALU.mult, op1=ALU.add), reads=[XRESb[x], PBb[bk]], writes=[ACCb[dc]])
        ln_accum(dc)

    def ln2_chunk(dc, bs, bq, tsl):
        x = dc % 2
        ln_norm(dc, bs, bq)
        P.op("act", lambda e: e.activation(out=OST[x][:, :], in_=ACC[:, dc, :], func=AF.Identity,
                                           scale=CST[:, C_LN2G + dc:C_LN2G + dc + 1], bias=CST[:, C_LN2B + dc:C_LN2B + dc + 1]),
             reads=[ACCb[dc]] + KR, writes=[OSTb[x]])
        P.dma("act", [lambda e: e.dma_start(out=ytv[:, dc, tsl], in_=OST[x][:, :])], reads=[OSTb[x]], writes=[], owner=OSTb[x])

    for grp in range(2):
        tsl = slice(grp * 512, (grp + 1) * 512)
        if grp == 0:
            for dc in range(32):
                c1_chunk(dc, tsl)
        bs, bq = ln_stats_acc()
        for dc in range(32):
            ln_norm(dc, bs, bq)
            P.op("act", lambda e, dc=dc: e.activation(out=MH[:, dc, :], in_=ACC[:, dc, :], func=AF.Identity,
                                                      scale=CST[:, C_LN1G + dc:C_LN1G + dc + 1], bias=CST[:, C_LN1B + dc:C_LN1B + dc + 1]),
                 reads=[ACCb[dc]] + KR, writes=[MHc[dc]])
            P.op("act", lambda e, dc=dc: e.activation(out=ACC[:, dc, :], in_=ACC[:, dc, :], func=AF.Identity,
                                                      scale=AG1[:, dc:dc + 1], bias=AB1[:, dc:dc + 1]),
                 reads=[ACCb[dc]] + KR, writes=[ACCb[dc]])
        sbs = [list(range(i, min(i + FBG, NFB))) for i in range(0, NFB, FBG)]

        def GU(si):
            hx = si % 2
            fbs = sbs[si]
            fi0 = 0
            if si == 0:
                sl4, bk4 = [], []
                for fb in fbs[:2]:
                    for w in (W_GT, W_UP):
                        sl4.append(load_slot(w + fb))
                        bk4.append(next_pb())
                for kc in range(32):
                    for s, bk in zip(sl4, bk4):
                        P.op("pe", lambda e, s=s, bk=bk, kc=kc: e.matmul(PBt[bk][:, :], lhsT=SL[s][:, kc * 128:(kc + 1) * 128], rhs=MH[:, kc, :],
                                                                         start=(kc == 0), stop=(kc == 31)),
                             reads=[MHc[kc], SLb[s]], writes=[PBb[bk]])
                for fi, fb in enumerate(fbs[:2]):
                    x = fb % 2
                    bg, bu = bk4[2 * fi], bk4[2 * fi + 1]
                    P.op("act", lambda e, bk=bg, x=x: e.activation(out=SGC[x][:, :], in_=PBt[bk][:, :], func=AF.Silu), reads=[PBb[bg]], writes=[SGCb[x]])
                    P.op("dve", lambda e, bk=bu, x=x, fi=fi, hx=hx: e.tensor_tensor(out=HID[hx][:, fi, :], in0=SGC[x][:, :], in1=PBt[bk][:, :], op=ALU.mult),
                         reads=[SGCb[x], PBb[bu]], writes=[HIDb[hx]])
                fi0 = 2
            for fi, fb in list(enumerate(fbs))[fi0:]:
                s_g = load_slot(W_GT + fb)
                s_u = load_slot(W_UP + fb)
                bks = []
                for s in (s_g, s_u):
                    bk = next_pb()
                    bks.append(bk)

                    def fn(e, bk=bk, s=s):
                        for kc in range(32):
                            ins = e.matmul(PBt[bk][:, :], lhsT=SL[s][:, kc * 128:(kc + 1) * 128], rhs=MH[:, kc, :], start=(kc == 0), stop=(kc == 31))
                        return ins
                    P.op("pe", fn, reads=MHc + [SLb[s]], writes=[PBb[bk]])
                x = fb % 2
                P.op("act", lambda e, bk=bks[0], x=x: e.activation(out=SGC[x][:, :], in_=PBt[bk][:, :], func=AF.Silu), reads=[PBb[bks[0]]], writes=[SGCb[x]])
                P.op("dve", lambda e, bk=bks[1], x=x, fi=fi, hx=hx: e.tensor_tensor(out=HID[hx][:, fi, :], in0=SGC[x][:, :], in1=PBt[bk][:, :], op=ALU.mult),
                     reads=[SGCb[x], PBb[bks[1]]], writes=[HIDb[hx]])

        def DOWN(si):
            hx = si % 2
            fbs = sbs[si]
            slots = [load_slot(W_DW + fb) for fb in fbs]
            for dc0 in range(0, 32, 2):
                bks = [next_pb(), next_pb()]

                def fn(e, bks=bks, dc0=dc0):
                    for d in range(2):
                        dc = dc0 + d
                        for fi in range(len(fbs)):
                            ins = e.matmul(PBt[bks[d]][:, :], lhsT=SL[slots[fi]][:, dc * 128:(dc + 1) * 128], rhs=HID[hx][:, fi, :],
                                           start=(fi == 0), stop=(fi == len(fbs) - 1))
                    return ins
                P.op("pe", fn, reads=[HIDb[hx]] + [SLb[s] for s in slots], writes=[PBb[bks[0]], PBb[bks[1]]])
                for d in range(2):
                    P.op("dve", lambda e, bk=bks[d], dc=dc0 + d: e.tensor_tensor(out=ACC[:, dc, :], in0=ACC[:, dc, :], in1=PBt[bk][:, :], op=ALU.add),
                         reads=[ACCb[dc0 + d], PBb[bks[d]]], writes=[ACCb[dc0 + d]])

        for si in range(len(sbs) + 1):
            if si < len(sbs):
                GU(si)
            if si >= 1:
                DOWN(si - 1)
        bs, bq = ln_stats_mm()
        if grp == 0:
            tsl1 = slice(512, 1024)
            mh_load(tsl1)
            pinned.update((bs, bq))
            for dc in range(32):
                ln2_chunk(dc, bs, bq, tsl)
                c1_chunk(dc, tsl1)
            pinned.clear()
        else:
            for dc in range(32):
                ln2_chunk(dc, bs, bq, tsl)
    P.barrier()
    return finish(nc, P)


def finish(nc, P):
    P.barrier(pool=True)
    with nc.Block() as block:
        @block.tensor
        def _(e):
            P.replay("pe", e)

        @block.scalar
        def _(e):
            P.replay("act", e)

        @block.vector
        def _(e):
            P.replay("dve", e)

        @block.gpsimd
        def _(e):
            P.replay("pool", e)

        @block.sync
        def _(e):
            P.replay("sp", e)
    return nc


def build_wstream(w_in, w_o, w_gate, w_up, w_down):
    wst = np.empty((NSLOT, 128, 4096), dtype=np.float32)

    def put_x(base, W, col0):
        Pn = W[:, col0:col0 + 512].reshape(32, 128, 512)
        for kq in range(4):
            wst[base + kq] = Pn[kq * 8:(kq + 1) * 8].transpose(1, 0, 2).reshape(128, 4096)

    def put_s(idx, W, col0):
        wst[idx] = W[:, col0:col0 + 128].reshape(32, 128, 128).transpose(1, 0, 2).reshape(128, 4096)

    for hp in range(4):
        put_x(W_Q + 4 * hp, w_in, 512 * hp)
        put_x(W_K + 4 * hp, w_in, 2048 + 512 * hp)
        put_x(W_V + 4 * hp, w_in, 4096 + 512 * hp)
    for g in range(16):
        put_s(W_A + g, w_in, 6144 + 128 * g)
        put_s(W_G + g, w_in, 6144 + 2048 + 128 * g)
    for dc in range(32):
        put_s(W_O + dc, w_o, 128 * dc)
    for fb in range(NFB):
        put_s(W_GT + fb, w_gate, 128 * fb)
        put_s(W_UP + fb, w_up, 128 * fb)
        wst[W_DW + fb] = w_down[fb * 128:(fb + 1) * 128, :]
    return wst.reshape(NSLOT * 128, 4096)


def col32(v):
    return np.ascontiguousarray(v.reshape(-1, 128).T)


def prep_inputs(inp):
    x = np.asarray(inp["x"], dtype=np.float32)
    pos = np.asarray(inp["positions"]).astype(np.int32)
    wst = build_wstream(np.asarray(inp["w_in"][0]), np.asarray(inp["w_o"][0]), np.asarray(inp["w_gate"][0]),
                        np.asarray(inp["w_up"][0]), np.asarray(inp["w_down"][0]))
    cst0 = np.zeros((128, NCST), dtype=np.float32)
    cst0[:, C_IDENT:C_IDENT + 128] = np.eye(128, dtype=np.float32)
    cst0[:, C_TRI:C_TRI + 128] = np.triu(np.ones((128, 128), dtype=np.float32))
    cst0[:, C_BGLU:C_BGLU + 32] = col32(np.asarray(inp["b_glu"][0]))
    cw = np.asarray(inp["conv_w"][0])
    cst0[:, C_CONVW:C_CONVW + 496] = cw.reshape(31, 16, 128).transpose(2, 1, 0).reshape(128, 496)
    cst0[:, C_CONVB:C_CONVB + 16] = col32(np.asarray(inp["conv_b"][0]))
    cst0[:, C_CLNG:C_CLNG + 16] = col32(np.asarray(inp["conv_ln_g"][0]))
    cst0[:, C_CLNB:C_CLNB + 16] = col32(np.asarray(inp["conv_ln_b"][0]))
    cst0[:, C_LN1G:C_LN1G + 32] = col32(np.asarray(inp["ln1_g"][0]))
    cst0[:, C_LN1B:C_LN1B + 32] = col32(np.asarray(inp["ln1_b"][0]))
    cst0[:, C_LN2G:C_LN2G + 32] = col32(np.asarray(inp["ln2_g"][0]))
    cst0[:, C_LN2B:C_LN2B + 32] = col32(np.asarray(inp["ln2_b"][0]))
    for k, nm in enumerate(("lam_q1", "lam_k1", "lam_q2", "lam_k2")):
        cst0[:, C_LAMV + 128 * k:C_LAMV + 128 * (k + 1)] = np.asarray(inp[nm][0])[None, :]
    cst0[:, C_SUBG:C_SUBG + 256] = np.asarray(inp["subln_g"][0])[None, :]
    cst0[:, C_INVF:C_INVF + 16] = (500000.0 ** (-np.arange(0, 32, 2, dtype=np.float32) / 32.0)).astype(np.float32)[None, :]
    in_maps = []
    for c in range(8):
        b, hf = c // 2, c % 2
        own = x[b, hf * NT:(hf + 1) * NT, :]
        xto = np.ascontiguousarray(own.T)
        if hf == 1:
            xtp = np.ascontiguousarray(x[b, 0:NT, :].T)
        else:
            xtp = np.zeros((D, NT), dtype=np.float32)
        xth = np.ascontiguousarray(xtp[:, NT - 32:])
        posi = np.empty((128, 16), dtype=np.int32)
        pprev = pos[b, 0:NT] if hf == 1 else pos[b, 0:NT]
        posi[:, 0:8] = pprev.reshape(8, 128).T
        posi[:, 8:16] = pos[b, hf * NT:(hf + 1) * NT].reshape(8, 128).T
        cst = cst0.copy()
        cst[:, C_FLAGS] = float(hf)
        cst[:, C_FLAGS + 1] = 1.0
        in_maps.append({"wst": wst, "xto": xto, "xtp": xtp, "xth": xth, "posi": posi, "cst": cst})
    return in_maps


def kernel(**inputs):
    in_maps = prep_inputs(inputs)
    nc = build_program()
    res = run_bass_kernel_spmd(nc, in_maps, core_ids=list(range(8)))
    out = np.empty((B, S, D), dtype=np.float32)
    for c in range(8):
        b, hf = c // 2, c % 2
        out[b, hf * NT:(hf + 1) * NT, :] = np.asarray(res.results[c]["yt"]).T
    return out
```
